# Optimizing a Trainium2 kernel written in Bass

```python
import math
import jax, jax.numpy as jnp
from jax import lax
import numpy as np

D_MODEL = 1024
BATCH = 16
SEQ = 2048
DEPTH = 1
DEC_BATCH = 16
DEC_SEQ = 64
PAST_LEN = 4096

CHUNK = 64
N_MEM = 256
H_A = 8
DH_A = 64
BAND_CHUNKS = 8
WIN_A = BAND_CHUNKS * CHUNK
REL_CLIP = 128
H_B = 4
DH_B = 64
H_M = 4
DH_M = 128
D_A = H_A * DH_A
D_B = H_B * 2 * DH_B
D_M = H_M * DH_M
N_BRANCH = 3
D_IN = 3 * D_A + 3 * D_B + D_M + N_BRANCH * D_MODEL
D_FF = 2816
CONV_W = 3
Q_BLOCK = 128
EPS = 1e-6
NEG_INF = -1e30

kernel_name = "hybrid_streaming_encoder_step"


def _rmsnorm(x, g):
    xf = x.astype(jnp.float32)
    y = xf * lax.rsqrt(jnp.mean(xf * xf, axis=-1, keepdims=True) + EPS)
    return (y * g.astype(jnp.float32)).astype(x.dtype)


def _lambda_init(layer):
    return 0.8 - 0.6 * math.exp(-0.3 * layer)


def _in_proj(h, w_in, b_gate):
    B, T, _ = h.shape
    z = h @ w_in
    bounds = [D_A, 2 * D_A, 3 * D_A, 3 * D_A + D_B, 3 * D_A + 2 * D_B, 3 * D_A + 3 * D_B, 3 * D_A + 3 * D_B + D_M]
    qa, ka, va, qb, kb, vb, qm, gl = jnp.split(z, bounds, axis=-1)
    qa = qa.reshape(B, T, H_A, DH_A)
    ka = ka.reshape(B, T, H_A, DH_A)
    va = va.reshape(B, T, H_A, DH_A)
    qb = qb.reshape(B, T, H_B, 2, DH_B)
    kb = kb.reshape(B, T, H_B, 2, DH_B)
    vb = vb.reshape(B, T, H_B, 2 * DH_B)
    qm = qm.reshape(B, T, H_M, DH_M)
    gates = jax.nn.sigmoid((gl + b_gate).astype(jnp.float32)).astype(h.dtype)
    return qa, ka, va, qb, kb, vb, qm, gates.reshape(B, T, N_BRANCH, D_MODEL)


def _rel_bias(rel_bias, dist):
    idx = jnp.clip(dist, -REL_CLIP, REL_CLIP) + REL_CLIP
    return rel_bias.astype(jnp.float32)[:, idx]


def _band_attn_prompt(q, k, v, rel_bias):
    B, S = q.shape[:2]
    nc = S // CHUNK
    nk = (BAND_CHUNKS + 1) * CHUNK
    pad = ((0, 0), (WIN_A, 0), (0, 0), (0, 0))
    kp = jnp.pad(k, pad).reshape(B, nc + BAND_CHUNKS, CHUNK, H_A, DH_A)
    vp = jnp.pad(v, pad).reshape(B, nc + BAND_CHUNKS, CHUNK, H_A, DH_A)
    kband = jnp.concatenate([kp[:, j:j + nc] for j in range(BAND_CHUNKS + 1)], axis=2)
    vband = jnp.concatenate([vp[:, j:j + nc] for j in range(BAND_CHUNKS + 1)], axis=2)
    qc = q.reshape(B, nc, CHUNK, H_A, DH_A)
    qi = jnp.arange(CHUNK)
    kj = jnp.arange(nk)
    bias = _rel_bias(rel_bias, qi[:, None] + WIN_A - kj[None, :])
    key_pos = (jnp.arange(nc)[:, None] - BAND_CHUNKS) * CHUNK + kj[None, :]
    valid = key_pos >= 0
    s = jnp.einsum('bnqhd,bnkhd->bnhqk', qc, kband).astype(jnp.float32) * (DH_A ** -0.5) + bias[None, None]
    s = jnp.where(valid[None, :, None, None, :], s, NEG_INF)
    p = jax.nn.softmax(s, axis=-1).astype(v.dtype)
    o = jnp.einsum('bnhqk,bnkhd->bnqhd', p, vband)
    return o.reshape(B, S, D_A)


def _band_attn_sample(q, k_new, v_new, cache_k, cache_v, rel_bias):
    B, T = q.shape[:2]
    L = cache_k.shape[1]
    kk = jnp.concatenate([cache_k, k_new], axis=1)
    vv = jnp.concatenate([cache_v, v_new], axis=1)
    bias = _rel_bias(rel_bias, jnp.arange(T)[:, None] + L - jnp.arange(L + T)[None, :])
    s = jnp.einsum('bqhd,bkhd->bhqk', q, kk).astype(jnp.float32) * (DH_A ** -0.5) + bias[None]
    p = jax.nn.softmax(s, axis=-1).astype(vv.dtype)
    o = jnp.einsum('bhqk,bkhd->bqhd', p, vv)
    return o.reshape(B, T, D_A), kk[:, T:], vv[:, T:]


def _diff_attend(q, k, v, pos_q, pos_k, lam, slopes):
    s = jnp.einsum('bqhmd,bkhmd->bhmqk', q, k).astype(jnp.float32) * (DH_B ** -0.5)
    dist = jnp.abs(pos_q[:, None] - pos_k[None, :]).astype(jnp.float32)
    allowed = (pos_k[None, :] // CHUNK) <= (pos_q[:, None] // CHUNK)
    s = jnp.where(allowed, s - slopes[:, None, None, None] * dist, NEG_INF)
    p = jax.nn.softmax(s, axis=-1)
    a = (p[:, :, 0] - lam * p[:, :, 1]).astype(v.dtype)
    return jnp.einsum('bhqk,bkhe->bqhe', a, v)


def _diff_attn_prompt(q, k, v, lam, slopes):
    B, S = q.shape[:2]
    nb = S // Q_BLOCK
    qblk = jnp.moveaxis(q.reshape(B, nb, Q_BLOCK, H_B, 2, DH_B), 1, 0)
    pos_k = jnp.arange(S)

    def one_block(args):
        qb, bi = args
        return _diff_attend(qb, k, v, bi * Q_BLOCK + jnp.arange(Q_BLOCK), pos_k, lam, slopes)

    o = lax.map(one_block, (qblk, jnp.arange(nb)))
    return jnp.moveaxis(o, 0, 1).reshape(B, S, H_B, 2 * DH_B)


def _diff_out(o, g_sub, lam_init):
    B, T = o.shape[:2]
    return (_rmsnorm(o, g_sub) * (1.0 - lam_init)).reshape(B, T, D_B)


def _mem_kv(mem, g_mem, w_mem_kv):
    B, M, _ = mem.shape
    kv = _rmsnorm(mem, g_mem) @ w_mem_kv
    mk, mv = jnp.split(kv, 2, axis=-1)
    return mk.reshape(B, M, H_M, DH_M), mv.reshape(B, M, H_M, DH_M)


def _mem_attend(q, mk, mv):
    B, T = q.shape[:2]
    s = jnp.einsum('bqhd,bmhd->bhqm', q, mk).astype(jnp.float32) * (DH_M ** -0.5)
    p = jax.nn.softmax(s, axis=-1).astype(mv.dtype)
    return jnp.einsum('bhqm,bmhd->bqhd', p, mv).reshape(B, T, D_M)


def _merge(oa, ob, om, gates, w_oa, w_ob, w_om, w_out):
    m = gates[:, :, 0] * (oa @ w_oa) + gates[:, :, 1] * (ob @ w_ob) + gates[:, :, 2] * (om @ w_om)
    return m @ w_out


def _conv_ffn(h, prev, w_up, w_conv, b_conv, w_down):
    T = h.shape[1]
    a, g = jnp.split(h @ w_up, 2, axis=-1)
    ext = jnp.concatenate([prev.astype(a.dtype), a], axis=1)
    c = b_conv
    for j in range(CONV_W):
        c = c + ext[:, j:j + T] * w_conv[j]
    y = (jax.nn.gelu(c) * g) @ w_down
    return y, ext[:, -(CONV_W - 1):]


def setup_inputs(seed: int = 0) -> dict:
    key = jax.random.key(seed)
    ks = iter(jax.random.split(key, 40))

    def nrm(shape, scale=1.0):
        return jax.random.normal(next(ks), shape, jnp.float32) * scale

    a_buf = min(WIN_A, PAST_LEN)
    return {
        "x_prompt": nrm((BATCH, SEQ, D_MODEL)),
        "x_sample": nrm((DEC_BATCH, DEC_SEQ, D_MODEL)),
        "mem_prompt": nrm((BATCH, N_MEM, D_MODEL)),
        "cache_a_k": nrm((DEPTH, DEC_BATCH, a_buf, H_A, DH_A)),
        "cache_a_v": nrm((DEPTH, DEC_BATCH, a_buf, H_A, DH_A)),
        "cache_b_k": nrm((DEPTH, DEC_BATCH, PAST_LEN, H_B, 2 * DH_B)),
        "cache_b_v": nrm((DEPTH, DEC_BATCH, PAST_LEN, H_B, 2 * DH_B)),
        "cache_mem_k": nrm((DEPTH, DEC_BATCH, N_MEM, H_M, DH_M)),
        "cache_mem_v": nrm((DEPTH, DEC_BATCH, N_MEM, H_M, DH_M)),
        "state_ffn_conv": nrm((DEPTH, DEC_BATCH, CONV_W - 1, D_FF)),
        "g_mix": 1.0 + nrm((DEPTH, D_MODEL), 0.02),
        "w_in": nrm((DEPTH, D_MODEL, D_IN), D_MODEL ** -0.5),
        "b_gate": nrm((DEPTH, N_BRANCH * D_MODEL), 0.02),
        "rel_bias": nrm((DEPTH, H_A, 2 * REL_CLIP + 1), 0.1),
        "lam_q": nrm((DEPTH, 2, DH_B), 0.1),
        "lam_k": nrm((DEPTH, 2, DH_B), 0.1),
        "g_sub": 1.0 + nrm((DEPTH, 2 * DH_B), 0.02),
        "g_mem": 1.0 + nrm((DEPTH, D_MODEL), 0.02),
        "w_mem_kv": nrm((DEPTH, D_MODEL, 2 * D_M), D_MODEL ** -0.5),
        "w_oa": nrm((DEPTH, D_A, D_MODEL), D_A ** -0.5),
        "w_ob": nrm((DEPTH, D_B, D_MODEL), D_B ** -0.5),
        "w_om": nrm((DEPTH, D_M, D_MODEL), D_M ** -0.5),
        "w_out": nrm((DEPTH, D_MODEL, D_MODEL), D_MODEL ** -0.5),
        "g_ffn": 1.0 + nrm((DEPTH, D_MODEL), 0.02),
        "w_up": nrm((DEPTH, D_MODEL, 2 * D_FF), D_MODEL ** -0.5),
        "w_conv": nrm((DEPTH, CONV_W, D_FF), CONV_W ** -0.5),
        "b_conv": nrm((DEPTH, D_FF), 0.02),
        "w_down": nrm((DEPTH, D_FF, D_MODEL), D_FF ** -0.5),
        "g_final": 1.0 + nrm((D_MODEL,), 0.02),
    }


def reference(x_prompt, x_sample, mem_prompt, cache_a_k, cache_a_v, cache_b_k, cache_b_v,
              cache_mem_k, cache_mem_v, state_ffn_conv, g_mix, w_in, b_gate, rel_bias,
              lam_q, lam_k, g_sub, g_mem, w_mem_kv, w_oa, w_ob, w_om, w_out, g_ffn,
              w_up, w_conv, b_conv, w_down, g_final):
    slopes = jnp.exp2(-8.0 * jnp.arange(1, H_B + 1, dtype=jnp.float32) / H_B)
    xp, xs = x_prompt, x_sample
    B, S = xp.shape[:2]
    Bd, T = xs.shape[:2]
    L = cache_b_k.shape[2]
    a_keep = min(WIN_A, S)
    pa_k, pa_v, pb_k, pb_v, pm_k, pm_v, pconv = [], [], [], [], [], [], []
    sa_k, sa_v, sb_k, sb_v, sconv = [], [], [], [], []
    for l in range(DEPTH):
        lam_init = _lambda_init(l)
        lq = lam_q[l].astype(jnp.float32)
        lk = lam_k[l].astype(jnp.float32)
        lam = jnp.exp(jnp.sum(lq[0] * lk[0])) - jnp.exp(jnp.sum(lq[1] * lk[1])) + lam_init

        h = _rmsnorm(xp, g_mix[l])
        qa, ka, va, qb, kb, vb, qm, gates = _in_proj(h, w_in[l], b_gate[l])
        oa = _band_attn_prompt(qa, ka, va, rel_bias[l])
        ob = _diff_out(_diff_attn_prompt(qb, kb, vb, lam, slopes), g_sub[l], lam_init)
        mk, mv = _mem_kv(mem_prompt, g_mem[l], w_mem_kv[l])
        om = _mem_attend(qm, mk, mv)
        xp = xp + _merge(oa, ob, om, gates, w_oa[l], w_ob[l], w_om[l], w_out[l])
        f, conv_buf = _conv_ffn(_rmsnorm(xp, g_ffn[l]), jnp.zeros((B, CONV_W - 1, D_FF), xp.dtype),
                                w_up[l], w_conv[l], b_conv[l], w_down[l])
        xp = xp + f
        pa_k.append(ka[:, S - a_keep:])
        pa_v.append(va[:, S - a_keep:])
        pb_k.append(kb.reshape(B, S, H_B, 2 * DH_B))
        pb_v.append(vb)
        pm_k.append(mk)
        pm_v.append(mv)
        pconv.append(conv_buf)

        h = _rmsnorm(xs, g_mix[l])
        qa, ka, va, qb, kb, vb, qm, gates = _in_proj(h, w_in[l], b_gate[l])
        oa, new_ak, new_av = _band_attn_sample(qa, ka, va, cache_a_k[l], cache_a_v[l], rel_bias[l])
        kk = jnp.concatenate([cache_b_k[l].reshape(Bd, L, H_B, 2, DH_B), kb], axis=1)
        vv = jnp.concatenate([cache_b_v[l], vb], axis=1)
        ob = _diff_attend(qb, kk, vv, L + jnp.arange(T), jnp.arange(L + T), lam, slopes)
        ob = _diff_out(ob, g_sub[l], lam_init)
        om = _mem_attend(qm, cache_mem_k[l], cache_mem_v[l])
        xs = xs + _merge(oa, ob, om, gates, w_oa[l], w_ob[l], w_om[l], w_out[l])
        f, conv_buf = _conv_ffn(_rmsnorm(xs, g_ffn[l]), state_ffn_conv[l],
                                w_up[l], w_conv[l], b_conv[l], w_down[l])
        xs = xs + f
        sa_k.append(new_ak)
        sa_v.append(new_av)
        sb_k.append(kb.reshape(Bd, T, H_B, 2 * DH_B))
        sb_v.append(vb)
        sconv.append(conv_buf)

    y_prompt = _rmsnorm(xp, g_final)
    y_sample = _rmsnorm(xs, g_final)
    return (y_prompt, y_sample,
            jnp.stack(pa_k), jnp.stack(pa_v), jnp.stack(pb_k), jnp.stack(pb_v),
            jnp.stack(pm_k), jnp.stack(pm_v), jnp.stack(pconv),
            jnp.stack(sa_k), jnp.stack(sa_v), jnp.stack(sb_k), jnp.stack(sb_v), jnp.stack(sconv))
```

```python
import numpy as np
from contextlib import ExitStack
import concourse.bass as bass
import concourse.mybir as mybir
from concourse.bass_utils import run_bass_kernel_spmd

F32 = mybir.dt.float32
BF16 = mybir.dt.bfloat16
AF = mybir.ActivationFunctionType
ALU = mybir.AluOpType
AX = mybir.AxisListType


class Tok:
    __slots__ = ("name", "w", "r")

    def __init__(self, name):
        self.name = name
        self.w = None
        self.r = {}


class Chan:
    __slots__ = ("name", "sem", "cnt", "last")

    def __init__(self, name):
        self.name = name
        self.sem = None
        self.cnt = 0
        self.last = None


class Op:
    __slots__ = ("eng", "fn", "deps", "marked", "chan", "sig", "n_dma", "lazy")


class Sched:
    def __init__(self, nc, es, marks=None):
        self.nc = nc
        self.es = es
        self.marks = marks
        self.E = {"pe": nc.tensor, "act": nc.scalar, "dve": nc.vector,
                  "pool": nc.gpsimd, "sp": nc.sync}
        self.ops = []
        self.chans = []
        self.n = 0
        if marks is not None:
            self.esem = {e: es.enter_context(nc.semaphore("se_" + e)) for e in self.E}
            self.cnt = {e: 0 for e in self.E}
            self.waited = {e: {} for e in self.E}

    def chan(self, name):
        c = Chan(name)
        self.chans.append(c)
        return c

    def op(self, eng, fn, reads=(), writes=(), chan=None, n_dma=1, extra_deps=(), serialize=True, lazy=False):
        o = Op()
        o.lazy = lazy
        o.eng = eng
        o.fn = None
        o.chan = chan
        o.deps = []
        o.marked = False
        o.sig = None
        o.n_dma = n_dma
        seen = set()
        is_dma = chan is not None

        def add(d, raw):
            if d is None or id(d) in seen:
                return
            if d.chan is None and not is_dma and d.eng == eng:
                if eng == "pe" or not raw:
                    return
            seen.add(id(d))
            o.deps.append(d)

        for t in reads:
            add(t.w, True)
        for t in writes:
            add(t.w, False)
            for r in t.r.values():
                add(r, False)
        for d in extra_deps:
            add(d, True)
        if is_dma and serialize:
            add(chan.last, True)
            chan.last = o
        for t in reads:
            t.r[id(chan) if is_dma else eng] = o
        for t in writes:
            t.w = o
            t.r = {}
        idx = self.n
        self.n += 1
        if self.marks is None:
            for d in o.deps:
                d.marked = True
            self.ops.append(o)
            o.deps = []
            return o
        e = self.E[eng]
        w = self.waited[eng]
        for d in o.deps:
            if d.lazy:
                key, sem, val = id(d.chan), d.chan.sem, d.chan.cnt
            else:
                key, sem, val = d.sig
            if w.get(key, 0) >= val:
                continue
            e.wait_ge(sem, val)
            w[key] = val
        o.deps = []
        ins = fn(e)
        if chan is not None:
            c = chan
            if c.sem is None:
                c.sem = self.es.enter_context(self.nc.semaphore("sc_" + c.name))
            lst = ins if isinstance(ins, (list, tuple)) else [ins]
            for i_ in lst:
                i_.then_inc(c.sem, 16)
                c.cnt += 16
            o.sig = (id(c), c.sem, c.cnt)
        elif self.marks[idx]:
            self.cnt[eng] += 1
            ins.then_inc(self.esem[eng], 1)
            o.sig = (eng, self.esem[eng], self.cnt[eng])
        return o

    def get_marks(self):
        return [o.marked for o in self.ops]

    def finish(self, final_wait_eng="sp"):
        if self.marks is None:
            return
        e = self.E[final_wait_eng]
        for c in self.chans:
            if c.sem is not None and c.cnt > 0:
                e.wait_ge(c.sem, c.cnt)


D = 1024
KD = 8
SEQ = 2048
TS = 512
NSTEP = SEQ // TS
DFF = 2816
NFF = 22
LPAST = 4096
EPS = 1e-6
LAM_INIT = 0.2
SLOPES = [2.0 ** (-8.0 * (h + 1) / 4.0) for h in range(4)]
RING_COLS = 5632
NRING = 3


def build_nc():
    nc1 = bass.Bass("TRN2", target_bir_lowering=False)
    with ExitStack() as es1:
        S1 = Sched(nc1, es1, None)
        build_into(nc1, es1, S1)
        marks = S1.get_marks()
    nc = bass.Bass("TRN2", target_bir_lowering=False)
    es = ExitStack()
    S = Sched(nc, es, marks)
    build_into(nc, es, S)
    assert S.n == len(marks)
    S.finish()
    return nc, es


STOP = [None]
ONLY = [None]


class _Stop(Exception):
    pass


def build_into(nc, es, S):
    try:
        _build_into(nc, es, S)
    except _Stop:
        pass


def _build_into(nc, es, S):
    op = S.op

    def ckpt(n):
        if STOP[0] is not None and STOP[0] == n:
            raise _Stop()

    def din(name, shape):
        return nc.dram_tensor(name, list(shape), F32, kind="ExternalInput").ap()

    def dout(name, shape):
        return nc.dram_tensor(name, list(shape), F32, kind="ExternalOutput").ap()

    def sb(name, shape, dt):
        return es.enter_context(nc.sbuf_tensor("s_" + name, list(shape), dt))

    xp = din("xp", [2, SEQ, D]); xs = din("xs", [2, 64, D]); memp = din("memp", [2, 256, D])
    cak = din("cak", [2, 512, 512]); cav = din("cav", [2, 512, 512])
    cbk = din("cbk", [2, LPAST, 512]); cbv = din("cbv", [2, LPAST, 512])
    cmk = din("cmk", [2, 256, 512]); cmv = din("cmv", [2, 256, 512])
    sconvT = din("sconvT", [128, 2, NFF, 2])
    w_in = din("w_in", [D, 6656]); w_mem = din("w_mem", [D, 1024])
    w_oa = din("w_oa", [512, D]); w_ob = din("w_ob", [512, D]); w_om = din("w_om", [512, D])
    w_out = din("w_out", [D, D]); w_up = din("w_up", [D, 2 * DFF]); w_down = din("w_down", [DFF, D])
    g4 = din("g4", [4, 128, D])
    bgT = din("bgT", [128, 24]); wcT = din("wcT", [128, NFF, 3]); bcT = din("bcT", [128, NFF])
    gsub = din("gsub", [128, 1]); lamq = din("lamq", [128, 128]); lamk = din("lamk", [128, 128])
    relT_d = din("relT", [128, 8, 640]); rbfar_d = din("rbfar", [128, 8])
    ident_d = din("ident", [128, 128]); kmq_d = din("kmq", [128, 128])
    posB_d = din("posB", [128, 16]); posS_d = din("posS", [128, 33])

    y_p = dout("y_p", [2, SEQ, D]); y_s = dout("y_s", [2, 64, D])
    pa_k = dout("pa_k", [2, 512, 512]); pa_v = dout("pa_v", [2, 512, 512])
    pb_k = dout("pb_k", [2, SEQ, 512]); pb_v = dout("pb_v", [2, SEQ, 512])
    pm_k = dout("pm_k", [2, 256, 512]); pm_v = dout("pm_v", [2, 256, 512])
    pconv = dout("pconv", [2, 2, DFF])
    sa_k = dout("sa_k", [2, 512, 512]); sa_v = dout("sa_v", [2, 512, 512])
    sb_k = dout("sb_k", [2, 64, 512]); sb_v = dout("sb_v", [2, 64, 512])
    sconv = dout("sconv", [2, 2, DFF])

    ident = sb("ident", [128, 128], BF16); ones = sb("ones", [128, 128], BF16)
    identf = sb("identf", [128, 128], F32)
    onesf = sb("onesf", [128, 128], F32)
    strip = sb("strip", [128, 8, 640], BF16)
    mcorr = sb("mcorr", [128, 4, 128], BF16)
    biasB = sb("biasB", [128, 4, 16], F32); biasS = sb("biasS", [128, 4, 33], F32)
    biasB0 = sb("biasB0", [128, 2, 16], F32)
    rbfar = sb("rbfar", [128, 8], F32)
    bg = sb("bg", [128, 24], F32); wc = sb("wc", [128, NFF, 3], F32); bc = sb("bc", [128, NFF], F32)
    gsubs = sb("gsubs", [128, 1], F32); nlam = sb("nlam", [128, 1], F32)
    gb = sb("gb", [128, 2, D], F32)
    kbT = sb("kbT", [128, 4, SEQ], BF16); vb = sb("vb", [128, 16, 512], BF16)
    kaT = sb("kaT", [128, 4, 1024], BF16); va = sb("va", [128, 8, 576], BF16)
    mkT = sb("mkT", [128, 4, 256], BF16); mv = sb("mv", [128, 2, 512], BF16)
    xt = sb("xt", [128, 4, D], F32)
    hT = sb("hT", [128, KD, TS], BF16)
    U = sb("U", [128, NFF, TS], BF16)
    obT = sb("obT", [128, 4, TS], BF16); omT = sb("omT", [128, 4, TS], BF16)
    ring = sb("ring", [128, NRING, RING_COLS], BF16)
    pT = sb("pT", [128, 4, 512], BF16)
    qzs = sb("qzs", [128, 8, 64], BF16)
    qz = sb("qz", [128, 2, 512], BF16)
    junk = pT[:, 0:2, :].rearrange("p a b -> p (a b)")
    wk = sb("wk", [128, 8, 512], F32)
    abuf = sb("abuf", [128, 3, 514], F32)
    ahist = sb("ahist", [128, 2, NFF, 2], F32)
    stg = sb("stg", [128, 3, 512], F32)
    xn = sb("xn", [128, 2, D], BF16)
    stat = sb("stat", [128, 64], F32)
    ktile = sb("ktile", [128, 2, 512], BF16)
    kTt = sb("kTt", [128, 2, 512], BF16)
    vtile = sb("vtile", [128, 2, 512], BF16)
    ps = [es.enter_context(nc.psum_tensor("ps%d" % i, [128, 512], F32)) for i in range(8)]
    psT = ps[7][:].bitcast(BF16)

    def toks(name, n):
        return [Tok("%s%d" % (name, i)) for i in range(n)]
    t_const = Tok("const")
    t_gb = toks("gb", 2); t_kbT = toks("kbT", 4); t_vb = toks("vb", 4); t_kaT = toks("kaT", 2); t_va = toks("va", 2)
    t_mkT = Tok("mkT"); t_mv = Tok("mv"); t_xt = toks("xt", 4); t_hT = Tok("hT"); t_U = toks("U", NFF)
    t_obT = toks("obT", 4); t_omT = toks("omT", 4); t_ring = toks("ring", NRING); t_pT = toks("pT", 4)
    t_wk = toks("wk", 8); t_abuf = toks("abuf", 3); t_ahist = Tok("ahist"); t_stg = toks("stg", 3)
    t_xn = toks("xn", 2); t_junk = Tok("junk"); t_stat = toks("stat", 64)
    t_ktile = toks("ktile", 2); t_kTt = toks("kTt", 2); t_vtile = toks("vtile", 2)
    t_ps = toks("ps", 8)
    t_qz = toks("qz", 2)
    t_qzs = Tok("qzs")
    c_ring = [S.chan("ring%d" % i) for i in range(NRING)]
    c_xt = [S.chan("xt%d" % i) for i in range(4)]
    c_stg = [S.chan("stg%d" % i) for i in range(3)]
    c_yst = [S.chan("yst%d" % i) for i in range(4)]; c_gb = [S.chan("gb%d" % i) for i in range(2)]
    c_const = S.chan("const"); c_misc = S.chan("misc"); c_constp = S.chan("constp"); c_miscp = S.chan("miscp")
    c_ktile = [S.chan("ktile%d" % i) for i in range(2)]; c_vtile = [S.chan("vtile%d" % i) for i in range(2)]
    c_d2d = S.chan("d2d")
    c_ktile2 = [S.chan("ktileh%d" % i) for i in range(2)]; c_vtile2 = [S.chan("vtileh%d" % i) for i in range(2)]

    rr = {"ring": 0, "wk": 0, "stat": 0, "pT": 0, "stg": 0, "xn": 0, "gb": 0, "abuf": 0, "ev": 0,
          "kt": 0, "vt": 0}

    def nxt(name, n):
        i = rr[name] % n
        rr[name] += 1
        return i

    def ev_eng():
        return "act" if nxt("ev", 2) == 0 else "dve"

    def copy_op(eng, out, in_, reads, writes):
        if eng == "act":
            return op("act", lambda e: e.activation(out=out, in_=in_, func=AF.Copy), reads=reads, writes=writes)
        return op(eng, lambda e: e.tensor_copy(out=out, in_=in_), reads=reads, writes=writes)

    def new_stat():
        i = nxt("stat", 64)
        return stat[:, i:i + 1], t_stat[i]

    def new_wk():
        i = nxt("wk", 8)
        return wk[:, i, :], t_wk[i]

    def w_cols(W, K, c0, ncol):
        return W.rearrange("(k p) c -> p k c", p=128)[:, :, c0:c0 + ncol]

    w_oa_v = w_oa.rearrange("(h p) c -> p h c", p=64)
    w_down_v = w_down.rearrange("(k p) c -> p k c", p=128)

    def slot_parts(key):
        kind, i = key
        if kind == "mem":
            return [(0, KD, 512, w_cols(w_mem, KD, i * 512, 512))]
        if kind == "in":
            return [(0, KD, 512, w_cols(w_in, KD, i * 512, 512))]
        if kind == "mg":
            f = i
            parts = [(0, 8, 128, w_oa_v[:, :, f * 128:(f + 1) * 128], 64),
                     (1024, 4, 128, w_cols(w_ob, 4, f * 128, 128)),
                     (1536, 4, 128, w_cols(w_om, 4, f * 128, 128))]
            for bb in range(3):
                cg = 3584 + bb * 1024 + f * 128
                parts.append((2048 + bb * 128, KD, 128, w_cols(w_in, KD, cg, 128), 128, 384))
            return parts
        if kind == "out":
            return [(0, KD, 512, w_cols(w_out, KD, i * 512, 512))]
        if kind == "up":
            return [(0, KD, 256, w_cols(w_up, KD, i * 256, 256)),
                    (2048, KD, 256, w_cols(w_up, KD, DFF + i * 256, 256))]
        if kind == "dn":
            return [(0, NFF, 256, w_down_v[:, :, i * 256:(i + 1) * 256])]
        raise KeyError(key)

    slot_keys = ([("mem", 0), ("mem", 1)] + [("in", g) for g in (1, 4, 2, 5, 0, 3, 6)] +
                 [("mg", f) for f in range(8)] + [("out", 0), ("out", 1)] +
                 [("up", i) for i in range(NFF // 2)] + [("dn", i) for i in range(4)])
    groups = {"mem": 0, "in": 1, "mg": 2, "out": 2, "up": 3, "dn": 4}
    wscr = nc.dram_tensor("wscr", [len(slot_keys), 128, RING_COLS], BF16, kind="Internal").ap()
    NCONV = 5
    AHEAD = 8
    c_conv = [S.chan("conv%d" % i) for i in range(NCONV)]
    slot_info = {}
    slot_index = {key: i for i, key in enumerate(slot_keys)}
    conv_state = {"next": 0}

    def issue_conv(upto):
        while conv_state["next"] <= min(upto, len(slot_keys) - 1):
            idx = conv_state["next"]
            conv_state["next"] += 1
            key = slot_keys[idx]
            parts = slot_parts(key)
            ext = 0
            for prt in parts:
                c0, K, ncol = prt[0], prt[1], prt[2]
                cs = prt[5] if len(prt) > 5 else ncol
                ext = max(ext, c0 + (K - 1) * cs + ncol)

            def fn(e, idx=idx, parts=parts):
                lst = []
                for prt in parts:
                    c0, K, ncol, src = prt[:4]
                    np_ = prt[4] if len(prt) > 4 else 128
                    cs = prt[5] if len(prt) > 5 else ncol
                    dst = wscr[idx, 0:np_, c0:c0 + K * cs].rearrange("p (k c) -> p k c", k=K)[:, :, 0:ncol]
                    lst.append(e.dma_start(out=dst, in_=src))
                return lst
            o = op("pool", fn, chan=c_conv[idx % NCONV])
            slot_info[key] = (idx, ext, o)

    issue_conv(AHEAD - 1)

    def wload(key):
        issue_conv(slot_index[key] + AHEAD)
        idx, ext, conv_op = slot_info[key]
        s = nxt("ring", NRING)
        if key[0] == "mg":
            regs = [(0, 64, 0, ext), (64, 128, 1024, ext)]
        else:
            regs = [(0, 128, 0, ext)]
        op("sp", lambda e: [e.dma_start(out=ring[p0:p1, s, c0:c1], in_=wscr[idx, p0:p1, c0:c1]) for (p0, p1, c0, c1) in regs],
           writes=[t_ring[s]], chan=c_ring[s], extra_deps=[conv_op])
        return s

    cb16 = nc.dram_tensor("cb16", [2, 2, LPAST, 512], BF16, kind="Internal").ap()
    c_cc = [S.chan("cc%d" % i) for i in range(4)]
    cc_ops = {}
    cc_list = [(kv, bb, ch) for bb in range(2) for ch in range(8) for kv in range(2)]
    cc_state = {"next": 0}

    def issue_cache_conv(n):
        for _ in range(n):
            if cc_state["next"] >= len(cc_list):
                return
            j = cc_state["next"]
            cc_state["next"] += 1
            kv, bb, ch = cc_list[j]
            src = (cbk if kv == 0 else cbv)[bb, ch * 512:(ch + 1) * 512, :]
            o = op("pool", lambda e, kv=kv, bb=bb, ch=ch, src=src: e.dma_start(out=cb16[kv, bb, ch * 512:(ch + 1) * 512, :], in_=src),
                   chan=c_cc[j % 4])
            cc_ops[(kv, bb, ch)] = o

    def wview(s, c0, K, ncol):
        return ring[:, s, c0:c0 + K * ncol].rearrange("p (k c) -> p k c", k=K)

    tmp_toks = []

    def cload(dst, src, queue="sp"):
        tok = Tok("c%d" % len(tmp_toks))
        ch = S.chan("cst%d" % len(tmp_toks))
        op(queue, lambda e: e.dma_start(out=dst, in_=src), writes=[tok], chan=ch)
        tmp_toks.append(tok)
        return tok

    cload(ident[:], ident_d[:, :], queue="pool")
    cload(identf[:], ident_d[:, :])
    cload(rbfar[:], rbfar_d[:, :])
    cload(bg[:], bgT[:, :]); cload(wc[:], wcT[:, :, :]); cload(bc[:], bcT[:, :])
    cload(gsubs[:], gsub[:, :])
    lq_w, lq_t = new_wk(); lk_w, lk_t = new_wk(); pos_w, pos_t = new_wk(); kq_w, kq_t = new_wk()
    t1 = cload(lq_w[:, 0:128], lamq[:, :]); t2 = cload(lk_w[:, 0:128], lamk[:, :])
    t3 = cload(pos_w[:, 0:16], posB_d[:, :]); t4 = cload(pos_w[:, 16:49], posS_d[:, :])
    t5 = cload(kq_w[:, 0:128], kmq_d[:, :])
    Uf = U[:].rearrange("p a b -> p (a b)").bitcast(F32)
    t6 = cload(Uf[:, 0:5120], relT_d.rearrange("p h x -> p (h x)"))
    op("dve", lambda e: e.memset(ones[:], 1.0), reads=tmp_toks, writes=[t_const, lq_t, lk_t, pos_t, kq_t] + t_U)
    op("pool", lambda e: e.memset(va[:, :, 512:576], 0.0), writes=[t_va[0], t_va[1]])
    op("pool", lambda e: e.memset(qz[:, :, :], 0.0), writes=[t_qz[0], t_qz[1]])
    op("pool", lambda e: e.memset(qzs[:, :, :], 0.0), writes=[t_qzs])
    op("dve", lambda e: e.memset(onesf[:], 1.0), writes=[t_const])

    def late_constants():
        op("dve", lambda e: e.tensor_scalar(out=gsubs[:], in0=gsubs[:], scalar1=1.0 - LAM_INIT, scalar2=None, op0=ALU.mult),
           reads=[t_const], writes=[t_const])
        op("dve", lambda e: e.tensor_tensor(out=lq_w[:, 0:128], in0=lq_w[:, 0:128], in1=lk_w[:, 0:128], op=ALU.mult),
           reads=[lq_t, lk_t], writes=[lq_t])
        s1, s1t = new_stat(); s2, s2t = new_stat()
        op("dve", lambda e: e.tensor_reduce(out=s1, in_=lq_w[:, 0:64], axis=AX.X, op=ALU.add), reads=[lq_t], writes=[s1t])
        op("dve", lambda e: e.tensor_reduce(out=s2, in_=lq_w[:, 64:128], axis=AX.X, op=ALU.add), reads=[lq_t], writes=[s2t])
        op("act", lambda e: e.activation(out=s1, in_=s1, func=AF.Exp), reads=[s1t], writes=[s1t])
        op("act", lambda e: e.activation(out=s2, in_=s2, func=AF.Exp), reads=[s2t], writes=[s2t])
        op("dve", lambda e: e.tensor_tensor(out=nlam[:], in0=s2, in1=s1, op=ALU.subtract), reads=[s1t, s2t], writes=[t_const])
        op("dve", lambda e: e.tensor_scalar(out=nlam[:], in0=nlam[:], scalar1=-LAM_INIT, scalar2=None, op0=ALU.add),
           reads=[t_const], writes=[t_const])
        for h in range(4):
            op("dve", lambda e, h=h: e.tensor_scalar(out=biasB[:, h, :], in0=pos_w[:, 0:16], scalar1=SLOPES[h], scalar2=None, op0=ALU.mult),
               reads=[pos_t], writes=[t_const])
            op("dve", lambda e, h=h: e.tensor_scalar(out=biasS[:, h, :], in0=pos_w[:, 16:49], scalar1=SLOPES[h], scalar2=None, op0=ALU.mult),
               reads=[pos_t], writes=[t_const])
        for hf_ in range(2):
            op("dve", lambda e, hf_=hf_: e.tensor_scalar(out=biasB0[:, hf_, :], in0=pos_w[:, 0:16], scalar1=SLOPES[0],
                                                         scalar2=SLOPES[0] * (128.0 if hf_ == 0 else -128.0), op0=ALU.mult, op1=ALU.add),
               reads=[pos_t], writes=[t_const])
        op("dve", lambda e: e.tensor_scalar(out=kq_w[:, 0:128], in0=kq_w[:, 0:128], scalar1=0.0, scalar2=None, op0=ALU.max),
           reads=[kq_t], writes=[kq_t])
        for h in range(4):
            op("act", lambda e, h=h: e.activation(out=mcorr[:, h, :], in_=kq_w[:, 0:128], func=AF.Exp, scale=-2.0 * SLOPES[h]),
               reads=[kq_t], writes=[t_const])
            op("dve", lambda e, h=h: e.memset(mcorr[64:128, h, 0:64], 0.0), reads=[t_const], writes=[t_const])
        nrb, nrbt = new_wk()
        op("dve", lambda e: e.tensor_scalar(out=nrb[:, 0:8], in0=rbfar[:], scalar1=-1.0, scalar2=None, op0=ALU.mult),
           reads=[t_const], writes=[nrbt])
        for h in range(8):
            op("act", lambda e, h=h: e.activation(out=strip[:, h, :], in_=Uf[:, h * 640:(h + 1) * 640], func=AF.Exp, bias=nrb[:, h:h + 1]),
               reads=t_U + [nrbt], writes=[t_const])

    xal = [obT[:].rearrange("p a b -> p (a b)").bitcast(F32), omT[:].rearrange("p a b -> p (a b)").bitcast(F32),
           stg[:, 0:2, :].rearrange("p a b -> p (a b)"), abuf[:].rearrange("p a b -> p (a b)")[:, 0:1024]]
    xal_tok = [t_obT, t_omT, [t_stg[0], t_stg[1]], t_abuf]
    c_xal = [S.chan("xal%d" % i) for i in range(4)]

    def load_gain(which):
        i = nxt("gb", 2)
        op("pool", lambda e: e.dma_start(out=gb[:, i, :], in_=g4[which, :, :]), writes=[t_gb[i]], chan=c_gb[i])
        return i

    def rms_stats(x_ap, tsz, xtok):
        xtoks = xtok if isinstance(xtok, list) else [xtok]
        ssq, ssqt = new_stat()
        op("act", lambda e: e.activation(out=junk[0:tsz, :], in_=x_ap, func=AF.Square, scale=1.0 / 32.0, accum_out=ssq[0:tsz, :]),
           reads=xtoks, writes=[t_pT[0], t_pT[1], ssqt])
        op("act", lambda e: e.activation(out=ssq[0:tsz, :], in_=ssq[0:tsz, :], func=AF.Ln, bias=EPS, scale=1.0),
           reads=[ssqt], writes=[ssqt])
        op("act", lambda e: e.activation(out=ssq[0:tsz, :], in_=ssq[0:tsz, :], func=AF.Exp, scale=-0.5),
           reads=[ssqt], writes=[ssqt])
        return ssq, ssqt

    def norm_prep(x_ap, tsz, xtok, gi):
        xtoks = xtok if isinstance(xtok, list) else [xtok]
        rstd, rt = rms_stats(x_ap, tsz, xtok)
        j = nxt("xn", 2)
        op("dve", lambda e: e.scalar_tensor_tensor(out=xn[0:tsz, j, :], in0=x_ap, scalar=rstd[0:tsz, :], in1=gb[0:tsz, gi, :],
                                                     op0=ALU.mult, op1=ALU.mult),
           reads=xtoks + [rt, t_gb[gi]], writes=[t_xn[j]])
        return j

    def norm_xpose(j, tsz, col0):
        for k in range(KD):
            op("pe", lambda e, k=k: e.transpose(out=psT[:, k * 128:k * 128 + tsz], in_=xn[0:tsz, j, k * 128:(k + 1) * 128],
                                                identity=ident[0:tsz, 0:tsz]),
               reads=[t_xn[j], t_const], writes=[t_ps[7]])
        src = psT.rearrange("p (k c) -> p k c", k=KD)[:, :, 0:tsz]
        copy_op(ev_eng(), hT[:, :, col0:col0 + tsz], src, [t_ps[7]], [t_hT])

    def norm_to_hT(x_ap, tsz, xtok, gi, col0):
        j = norm_prep(x_ap, tsz, xtok, gi)
        norm_xpose(j, tsz, col0)

    gemm_bank = [0]

    def nbank(pool=(0, 1, 2, 3, 4, 5, 6)):
        b = pool[gemm_bank[0] % len(pool)]
        gemm_bank[0] += 1
        return b

    def gemm_fm(bank, wv, c0, M, K, rhs_fn, ntok, rtoks, kpart=128):
        for k in range(K):
            op("pe", lambda e, k=k: e.matmul(ps[bank][0:M, 0:ntok], lhsT=wv[0:kpart, k, c0:c0 + M], rhs=rhs_fn(k),
                                             start=(k == 0), stop=(k == K - 1)),
               reads=rtoks, writes=[t_ps[bank]])

    def gemm_tm(bank, lhs_fn, tsz, wv, c0, ncol, K, rtoks, ktoks=None):
        for k in range(K):
            op("pe", lambda e, k=k: e.matmul(ps[bank][0:tsz, 0:ncol], lhsT=lhs_fn(k), rhs=wv[:, k, c0:c0 + ncol],
                                             start=(k == 0), stop=(k == K - 1)),
               reads=rtoks + ([ktoks[k]] if ktoks is not None else []), writes=[t_ps[bank]])

    def store_rows(bank, tsz, ncol, dst_ap, extra=None):
        i = nxt("stg", 3)
        copy_op(ev_eng(), stg[0:tsz, i, 0:ncol], ps[bank][0:tsz, 0:ncol], [t_ps[bank]], [t_stg[i]])
        if extra is not None:
            extra(stg[0:tsz, i, 0:ncol], t_stg[i])
        op("act", lambda e: e.dma_start(out=dst_ap, in_=stg[0:tsz, i, 0:ncol]), reads=[t_stg[i]], chan=c_stg[i])

    def mem_kv(b):
        gi = load_gain(1)
        for i in range(2):
            op("pool", lambda e, i=i: e.dma_start(out=xt[:, i, :], in_=memp[b, i * 128:(i + 1) * 128, :]),
               writes=[t_xt[i]], chan=c_xt[i])
            norm_to_hT(xt[:, i, :], 128, t_xt[i], gi, i * 128)
        for half in range(2):
            s = wload(("mem", half))
            wv = wview(s, 0, KD, 512)
            if half == 0:
                for c in range(4):
                    bk = nbank()
                    gemm_fm(bk, wv, c * 128, 128, KD, lambda k: hT[:, k, 0:256], 256, [t_ring[s], t_hT])
                    copy_op(ev_eng(), mkT[:, c, :], ps[bk][:, 0:256], [t_ps[bk]], [t_mkT])
            for i in range(2):
                bk = nbank()
                gemm_tm(bk, lambda k, i=i: hT[:, k, i * 128:(i + 1) * 128], 128, wv, 0, 512, KD, [t_ring[s], t_hT])
                dst = (pm_k if half == 0 else pm_v)[b, i * 128:(i + 1) * 128, :]
                if half == 1:
                    def extra(sap, stok, i=i):
                        op("pool", lambda e: e.tensor_copy(out=mv[:, i, :], in_=sap), reads=[stok], writes=[t_mv])
                    store_rows(bk, 128, 512, dst, extra)
                else:
                    store_rows(bk, 128, 512, dst)

    def act_recip(dst, dtok, src, stoks):
        op("act", lambda e: e.activation(out=dst, in_=src, func=AF.Ln), reads=stoks, writes=[dtok])
        op("act", lambda e: e.activation(out=dst, in_=dst, func=AF.Exp, scale=-1.0), reads=[dtok], writes=[dtok])

    def recip_mul(bank_o, bank_z, M, ncols, out_ap, out_tok, c0=0):
        rz, rzt = new_wk()
        act_recip(rz[0:M, 0:ncols], rzt, ps[bank_z][0:M, c0:c0 + ncols], [t_ps[bank_z]])
        op("dve", lambda e: e.tensor_tensor(out=out_ap, in0=ps[bank_o][0:M, c0:c0 + ncols], in1=rz[0:M, 0:ncols], op=ALU.mult),
           reads=[t_ps[bank_o], rzt], writes=[out_tok])

    def diff_finish(o0, o0t, o1, o1t, ncols, out_ap, out_tok):
        d, dt_ = new_wk()
        op("dve", lambda e: e.scalar_tensor_tensor(out=d[:, 0:ncols], in0=o1[:, 0:ncols], scalar=nlam[:, 0:1], in1=o0[:, 0:ncols],
                                                     op0=ALU.mult, op1=ALU.add),
           reads=[o0t, o1t, t_const], writes=[dt_])
        sq, sqt = new_wk()
        op("act", lambda e: e.activation(out=sq[:, 0:ncols], in_=d[:, 0:ncols], func=AF.Square), reads=[dt_], writes=[sqt])
        bk = 7
        op("pe", lambda e: e.matmul(ps[bk][:, 0:ncols], lhsT=onesf[:, :], rhs=sq[:, 0:ncols], start=True, stop=True),
           reads=[sqt, t_const], writes=[t_ps[bk]])
        op("act", lambda e: e.activation(out=sq[:, 0:ncols], in_=ps[bk][:, 0:ncols], func=AF.Ln, bias=EPS, scale=1.0 / 128.0),
           reads=[t_ps[bk]], writes=[sqt])
        op("act", lambda e: e.activation(out=sq[:, 0:ncols], in_=sq[:, 0:ncols], func=AF.Exp, scale=-0.5), reads=[sqt], writes=[sqt])
        op("dve", lambda e: e.scalar_tensor_tensor(out=out_ap, in0=d[:, 0:ncols], scalar=gsubs[:, 0:1], in1=sq[:, 0:ncols],
                                                     op0=ALU.mult, op1=ALU.mult),
           reads=[dt_, sqt, t_const], writes=[out_tok])


    qaT = lambda c: U[:, c, :]
    qbT = lambda h: U[:, 4 + h, :]
    qmT = lambda h: U[:, 8 + h, :]
    oaT = lambda h: U[0:64, 12 + h, :]

    def run_pipeline(items, depth=2):
        if not items:
            return
        pending = []
        for j in range(min(depth, len(items))):
            items[j][0]()
        for i in range(len(items)):
            if i + depth < len(items):
                items[i + depth][0]()
            fin = items[i][1]()
            while pending and pending[0][0] <= i:
                pending.pop(0)[1]()
            if fin is not None:
                pending.append((i + 2, fin))
        for _, fin in pending:
            fin()

    grp = {"copies": [], "done": set()}

    def emit_copy(i):
        if 0 <= i < len(grp["copies"]) and i not in grp["done"]:
            grp["done"].add(i)
            grp["copies"][i]()

    def items_B_prompt(t):
        items = []
        for h in range(4):
            maps = []
            for m in range(2):
                bo, bz = (0, 1) if (2 * h + m) % 2 == 0 else (2, 3)
                tiles = [(4 * t + j, 128 * j, True) for j in range(4)] + [(kt, 0, False) for kt in range(4 * t)]
                gi_ = len(grp["copies"])

                def cp(h=h, m=m):
                    pb = m * 64
                    op("dve", lambda e: e.tensor_copy(out=qz[pb:pb + 64, m, :], in_=U[pb:pb + 64, 4 + h, :]),
                       reads=[t_U[4 + h]], writes=[t_qz[m]])
                grp["copies"].append(cp)
                for n_, (kt, q0, diag) in enumerate(tiles):
                    stt = {}
                    first = (n_ == 0)
                    last = (n_ == len(tiles) - 1)

                    def qk(kt=kt, q0=q0, stt=stt, h=h, m=m, first=first, gi_=gi_):
                        bs = nbank((4, 5, 6))
                        stt["bs"] = bs
                        pb = m * 64
                        if first:
                            emit_copy(gi_)
                        op("pe", lambda e: e.matmul(
                            ps[bs][:, q0:TS], lhsT=kbT[:, h, kt * 128:(kt + 1) * 128], rhs=qz[:, m, q0:TS],
                            start=True, stop=True), reads=[t_kbT[kt // 4], t_qz[m]], writes=[t_ps[bs]])

                    def rest(kt=kt, q0=q0, diag=diag, stt=stt, h=h, m=m, first=first, last=last, bo=bo, bz=bz, maps=maps, gi_=gi_):
                        bs = stt["bs"]
                        if first:
                            emit_copy(gi_ + 1)
                        pi = nxt("pT", 4)
                        o_ = kt - 4 * t + 12
                        if h == 0:
                            segs_ = [(a_, b_, biasB0[:, i_, o_:o_ + 1]) for i_, (a_, b_) in
                                     enumerate(((q0, 256), (max(q0, 256), TS))) if b_ > a_]
                        else:
                            segs_ = [(q0, TS, biasB[:, h, o_:o_ + 1])]
                        for (a_, b_, bias_) in segs_:
                            op("act", lambda e, a_=a_, b_=b_, bias_=bias_: e.activation(
                                out=pT[:, pi, a_:b_], in_=ps[bs][:, a_:b_], func=AF.Exp, bias=bias_, scale=0.125),
                                reads=[t_ps[bs], t_const], writes=[t_pT[pi]])
                        if diag:
                            op("dve", lambda e: e.tensor_tensor(
                                out=pT[:, pi, q0:q0 + 128], in0=pT[:, pi, q0:q0 + 128], in1=mcorr[:, h, :], op=ALU.mult),
                                reads=[t_pT[pi], t_const], writes=[t_pT[pi]])
                        op("pe", lambda e: e.matmul(
                            ps[bo][:, q0:TS], lhsT=vb[:, kt, h * 128:(h + 1) * 128], rhs=pT[:, pi, q0:TS],
                            start=first, stop=last, skip_group_check=True), reads=[t_vb[kt // 4], t_pT[pi]], writes=[t_ps[bo]])
                        op("pe", lambda e: e.matmul(
                            ps[bz][:, q0:TS], lhsT=ones[:, :], rhs=pT[:, pi, q0:TS],
                            start=first, stop=last, skip_group_check=True), reads=[t_const, t_pT[pi]], writes=[t_ps[bz]])
                        if last:
                            def fin():
                                om_, omt = new_wk()
                                recip_mul(bo, bz, 128, TS, om_[:, :], omt)
                                maps.append((om_, omt))
                                if m == 1:
                                    diff_finish(maps[0][0], maps[0][1], maps[1][0], maps[1][1], TS, obT[:, h, :], t_obT[h])
                            return fin
                        return None
                    items.append((qk, rest))
        return items

    def items_A_prompt(t):
        items = []
        for h in range(8):
            c, pb = h // 2, (h % 2) * 64
            bo, bz = (0, 1) if h % 2 == 0 else (2, 3)
            js = [j for j in (4, 5, 6, 7, 3, 2, 1, 0) if 4 * t - 4 + j >= 0]
            gi_ = len(grp["copies"])

            def cp(h=h, c=c, pb=pb):
                par = h % 2
                op("dve", lambda e: e.tensor_copy(out=qz[pb:pb + 64, par, :], in_=U[pb:pb + 64, c, :]),
                   reads=[t_U[c]], writes=[t_qz[par]])
            grp["copies"].append(cp)
            for n_, j in enumerate(js):
                kt = 4 * t - 4 + j
                first = (n_ == 0)
                last = (n_ == len(js) - 1)
                if j <= 3:
                    q0, q1 = 0, 128 * (j + 1)
                else:
                    q0, q1 = 128 * (j - 4), TS
                stt = {}

                def qk(kt=kt, q0=q0, q1=q1, stt=stt, c=c, pb=pb, h=h, first=first, gi_=gi_):
                    rc = (kt % 8) * 128
                    bs = nbank((4, 5, 6))
                    stt["bs"] = bs
                    par = h % 2
                    if first:
                        emit_copy(gi_)
                    op("pe", lambda e: e.matmul(
                        ps[bs][:, q0:q1], lhsT=kaT[:, c, rc:rc + 128], rhs=qz[:, par, q0:q1],
                        start=True, stop=True), reads=[t_kaT[(kt // 4) % 2], t_qz[par]], writes=[t_ps[bs]])

                def rest(kt=kt, j=j, q0=q0, q1=q1, stt=stt, h=h, first=first, last=last, bo=bo, bz=bz, gi_=gi_):
                    bs = stt["bs"]
                    if first:
                        emit_copy(gi_ + 1)
                    pi = nxt("pT", 4)
                    op("act", lambda e: e.activation(
                        out=pT[:, pi, q0:q1], in_=ps[bs][:, q0:q1], func=AF.Exp, scale=0.125),
                        reads=[t_ps[bs]], writes=[t_pT[pi]])
                    if j >= 3:
                        x0 = q0 - 128 * (j - 4)
                        op("dve", lambda e: e.tensor_tensor(
                            out=pT[:, pi, q0:q1], in0=pT[:, pi, q0:q1], in1=strip[:, h, x0:x0 + (q1 - q0)], op=ALU.mult),
                            reads=[t_pT[pi], t_const], writes=[t_pT[pi]])
                    if j <= 3:
                        r0, cc = 0, 64 * (2 * j + 1)
                    else:
                        r0, cc = 64, 64 * (2 * j - 8)
                    op("dve" if j >= 3 else "pool", lambda e: e.memset(pT[r0:r0 + 64, pi, cc:cc + 64], 0.0),
                       reads=[t_pT[pi]], writes=[t_pT[pi]])
                    op("pe", lambda e: e.matmul(
                        ps[bo][:, q0:q1], lhsT=va[:, kt % 8, h * 64:h * 64 + 128], rhs=pT[:, pi, q0:q1],
                        start=first, stop=last, skip_group_check=True), reads=[t_va[(kt // 4) % 2], t_pT[pi]], writes=[t_ps[bo]])
                    op("pe", lambda e: e.matmul(
                        ps[bz][:, q0:q1], lhsT=ones[:, :], rhs=pT[:, pi, q0:q1],
                        start=first, stop=last, skip_group_check=True), reads=[t_const, t_pT[pi]], writes=[t_ps[bz]])
                    if last:
                        return lambda: recip_mul(bo, bz, 64, TS, U[0:64, 12 + h, :], t_U[12 + h])
                    return None
                items.append((qk, rest))
        return items

    def items_M(ntok, qc):
        items = []
        for h in range(4):
            bo, bz = (0, 1) if h % 2 == 0 else (2, 3)
            for mt in range(2):
                stt = {}

                def qk(mt=mt, stt=stt, h=h):
                    bs = nbank((4, 5, 6))
                    stt["bs"] = bs
                    op("pe", lambda e: e.matmul(
                        ps[bs][:, 0:ntok], lhsT=mkT[:, h, mt * 128:(mt + 1) * 128], rhs=U[:, 8 + h, qc:qc + ntok],
                        start=True, stop=True), reads=[t_mkT, t_U[8 + h]], writes=[t_ps[bs]])

                def rest(mt=mt, stt=stt, h=h, bo=bo, bz=bz):
                    bs = stt["bs"]
                    pi = nxt("pT", 4)
                    op("act", lambda e: e.activation(
                        out=pT[:, pi, 0:ntok], in_=ps[bs][:, 0:ntok], func=AF.Exp, scale=float(128 ** -0.5)),
                        reads=[t_ps[bs]], writes=[t_pT[pi]])
                    op("pe", lambda e: e.matmul(
                        ps[bo][:, 0:ntok], lhsT=mv[:, mt, h * 128:(h + 1) * 128], rhs=pT[:, pi, 0:ntok],
                        start=(mt == 0), stop=(mt == 1)), reads=[t_mv, t_pT[pi]], writes=[t_ps[bo]])
                    op("pe", lambda e: e.matmul(
                        ps[bz][:, 0:ntok], lhsT=ones[:, :], rhs=pT[:, pi, 0:ntok],
                        start=(mt == 0), stop=(mt == 1)), reads=[t_const, t_pT[pi]], writes=[t_ps[bz]])
                    if mt == 1:
                        return lambda: recip_mul(bo, bz, 128, ntok, omT[:, h, qc:qc + ntok], t_omT[h])
                    return None
                items.append((qk, rest))
        return items

    def load_ktile(src_rows, pre=None):
        ki = nxt("kt", 2)
        if pre is not None:
            op("sp", lambda e: e.dma_start(out=ktile[:, ki, :], in_=src_rows), writes=[t_ktile[ki]], chan=c_ktile2[ki],
               extra_deps=[pre])
        else:
            op("pool", lambda e: e.dma_start(out=ktile[:, ki, :], in_=src_rows), writes=[t_ktile[ki]], chan=c_ktile[ki])
        for c in range(4):
            op("pe", lambda e, c=c: e.transpose(out=psT[:, c * 128:(c + 1) * 128], in_=ktile[:, ki, c * 128:(c + 1) * 128],
                                                identity=ident[:, :]),
               reads=[t_ktile[ki], t_const], writes=[t_ps[7]])
        copy_op(ev_eng(), kTt[:, ki, :], psT[:, 0:512], [t_ps[7]], [t_kTt[ki]])
        return ki

    def load_vtile(src_rows, pre=None):
        vi = nxt("vt", 2)
        if pre is not None:
            op("pool", lambda e: e.dma_start(out=vtile[:, vi, :], in_=src_rows), writes=[t_vtile[vi]], chan=c_vtile[vi],
               extra_deps=[pre])
        else:
            op("pool", lambda e: e.dma_start(out=vtile[:, vi, :], in_=src_rows), writes=[t_vtile[vi]], chan=c_vtile[vi])
        return vi

    def attn_B_sample(b):
        qc = b * 64
        SP = (2, 3, 4, 5, 6)
        qzs_v = qzs[:].rearrange("p (h m) q -> p h m q", m=2)
        for m in range(2):
            op("pool", lambda e, m=m: e.tensor_copy(out=qzs_v[m * 64:(m + 1) * 64, :, m, :], in_=U[m * 64:(m + 1) * 64, 4:8, qc:qc + 64]),
               reads=t_U[4:8], writes=[t_qzs])
        items = []
        for kt in range(33):
            last = (kt == 32)
            nk = 64 if last else 128
            stt = {}

            def qk(kt=kt, last=last, nk=nk, stt=stt):
                if not last:
                    ki = load_ktile(cb16[0, b, kt * 128:(kt + 1) * 128, :], pre=cc_ops[(0, b, kt // 4)])
                    vi = load_vtile(cb16[1, b, kt * 128:(kt + 1) * 128, :], pre=cc_ops[(1, b, kt // 4)])
                    kT = lambda h: kTt[:, ki, h * 128:(h + 1) * 128]
                    vv = lambda h: vtile[:, vi, h * 128:(h + 1) * 128]
                    krd = [t_kTt[ki]]
                    vrd = [t_vtile[vi]]
                else:
                    kT = lambda h: kbT[:, h, qc:qc + 64]
                    vv = lambda h: vb[0:64, b, h * 128:(h + 1) * 128]
                    krd = [t_kbT[0]]
                    vrd = [t_vb[0]]
                bs = nbank(SP)
                for g in range(8):
                    op("pe", lambda e, g=g: e.matmul(
                        ps[bs][0:nk, g * 64:(g + 1) * 64], lhsT=kT(g // 2), rhs=qzs[:, g, :],
                        start=True, stop=True, skip_group_check=True), reads=krd + [t_qzs], writes=[t_ps[bs]])
                stt.update(bs=bs, vv=vv, vrd=vrd)

            def rest(kt=kt, last=last, nk=nk, stt=stt):
                bs, vv, vrd = stt["bs"], stt["vv"], stt["vrd"]
                pi = nxt("pT", 4)
                for h in range(4):
                    op("act", lambda e, h=h: e.activation(
                        out=pT[0:nk, pi, h * 128:(h + 1) * 128], in_=ps[bs][0:nk, h * 128:(h + 1) * 128], func=AF.Exp,
                        bias=biasS[0:nk, h, kt:kt + 1], scale=0.125), reads=[t_ps[bs], t_const], writes=[t_pT[pi]])
                if last:
                    for g in range(8):
                        op("dve", lambda e, g=g: e.tensor_tensor(
                            out=pT[0:64, pi, g * 64:(g + 1) * 64], in0=pT[0:64, pi, g * 64:(g + 1) * 64],
                            in1=mcorr[0:64, g // 2, 0:64], op=ALU.mult), reads=[t_pT[pi], t_const], writes=[t_pT[pi]])
                for g in range(8):
                    op("pe", lambda e, g=g: e.matmul(
                        ps[0][:, g * 64:(g + 1) * 64], lhsT=vv(g // 2), rhs=pT[0:nk, pi, g * 64:(g + 1) * 64],
                        start=(kt == 0 and g == 0), stop=last, skip_group_check=True),
                        reads=vrd + [t_pT[pi]], writes=[t_ps[0]])
                op("pe", lambda e: e.matmul(ps[1][:, 0:512], lhsT=ones[0:nk, :], rhs=pT[0:nk, pi, 0:512],
                                            start=(kt == 0), stop=last, skip_group_check=True),
                   reads=[t_const, t_pT[pi]], writes=[t_ps[1]])
                return None
            items.append((qk, rest))
        run_pipeline(items, depth=1)
        rz, rzt = new_wk()
        act_recip(rz[:, :], rzt, ps[1][:, 0:512], [t_ps[1]])
        on, ont = new_wk()
        op("dve", lambda e: e.tensor_tensor(out=on[:, :], in0=ps[0][:, 0:512], in1=rz[:, :], op=ALU.mult),
           reads=[t_ps[0], rzt], writes=[ont])
        for h in range(4):
            o0 = on[:, (2 * h) * 64:(2 * h + 1) * 64]
            o1 = on[:, (2 * h + 1) * 64:(2 * h + 2) * 64]
            diff_finish(o0, ont, o1, ont, 64, obT[:, h, qc:qc + 64], t_obT[h])

    def attn_A_sample(b):
        qc = b * 64
        for j in range(5):
            last = (j == 4)
            nk = 64 if last else 128
            if not last:
                ki = load_ktile(cak[b, j * 128:(j + 1) * 128, :])
                ckpt(40)
                vi = load_vtile(cav[b, j * 128:(j + 1) * 128, :])
                ckpt(41)
                kT = lambda h: kTt[(h % 2) * 64:(h % 2) * 64 + 64, ki, (h // 2) * 128:(h // 2 + 1) * 128]
                vv = lambda h: vtile[:, vi, h * 64:(h + 1) * 64]
                krd = [t_kTt[ki]]
                vrd = [t_vtile[vi]]
            else:
                kT = lambda h: kaT[(h % 2) * 64:(h % 2) * 64 + 64, h // 2, qc:qc + 64]
                vv = lambda h: va[0:64, b, h * 64:(h + 1) * 64]
                krd = [t_kaT[0]]
                vrd = [t_va[0]]
            bsp = (nbank((4, 5, 6)), nbank((4, 5, 6)))
            colA = lambda h: (h % 2) * 256 + (h // 2) * 64
            for h in range(8):
                pb = (h % 2) * 64
                op("pe", lambda e, h=h, pb=pb: e.matmul(
                    ps[bsp[h % 2]][0:nk, (h // 2) * 64:(h // 2 + 1) * 64], lhsT=kT(h), rhs=U[pb:pb + 64, h // 2, qc:qc + 64],
                    start=True, stop=True, skip_group_check=True), reads=krd + [t_U[h // 2]], writes=[t_ps[bsp[h % 2]]])
            pi = nxt("pT", 4)
            for par in range(2):
                op("act", lambda e, par=par: e.activation(out=pT[0:nk, pi, par * 256:(par + 1) * 256],
                                                          in_=ps[bsp[par]][0:nk, 0:256], func=AF.Exp, scale=0.125),
                   reads=[t_ps[bsp[par]]], writes=[t_pT[pi]])
            if j >= 3:
                x0 = 128 if j == 3 else 0
                for h in range(8):
                    op("dve", lambda e, h=h: e.tensor_tensor(
                        out=pT[0:nk, pi, colA(h):colA(h) + 64], in0=pT[0:nk, pi, colA(h):colA(h) + 64],
                        in1=strip[0:nk, h, x0:x0 + 64], op=ALU.mult), reads=[t_pT[pi], t_const], writes=[t_pT[pi]])
            for h in range(8):
                op("pe", lambda e, h=h: e.matmul(
                    ps[0][0:64, colA(h):colA(h) + 64], lhsT=vv(h), rhs=pT[0:nk, pi, colA(h):colA(h) + 64],
                    start=(j == 0 and h == 0), stop=last, skip_group_check=True), reads=vrd + [t_pT[pi]], writes=[t_ps[0]])
            op("pe", lambda e: e.matmul(ps[1][0:64, 0:512], lhsT=ones[0:nk, 0:64], rhs=pT[0:nk, pi, 0:512],
                                        start=(j == 0), stop=last, skip_group_check=True),
               reads=[t_const, t_pT[pi]], writes=[t_ps[1]])
        rz, rzt = new_wk()
        act_recip(rz[0:64, :], rzt, ps[1][0:64, 0:512], [t_ps[1]])
        for h in range(8):
            ca = (h % 2) * 256 + (h // 2) * 64
            op("dve", lambda e, h=h, ca=ca: e.tensor_tensor(out=U[0:64, 12 + h, qc:qc + 64], in0=ps[0][0:64, ca:ca + 64],
                                                             in1=rz[0:64, ca:ca + 64], op=ALU.mult),
               reads=[t_ps[0], rzt], writes=[t_U[12 + h]])

    def load_mem_sample(b):
        for mt in range(2):
            ki = load_ktile(cmk[b, mt * 128:(mt + 1) * 128, :])
            for h in range(4):
                copy_op("pool", mkT[:, h, mt * 128:(mt + 1) * 128], kTt[:, ki, h * 128:(h + 1) * 128], [t_kTt[ki]], [t_mkT])
            op("pool", lambda e, mt=mt: e.dma_start(out=mv[:, mt, :], in_=cmv[b, mt * 128:(mt + 1) * 128, :]),
               writes=[t_mv], chan=c_miscp)

    def step(kind, b, t, preloaded=False, next_x=None, gi_pre=None, has_next=False, norm_done=False):
        prm = (kind == "p")
        if prm:
            ntok = TS
            tiles = [(i * 128, 128) for i in range(4)]
            xsrc = lambda i: xp[b, t * TS + i * 128:t * TS + (i + 1) * 128, :]
            segs = [(0, TS, 0)]
        else:
            ntok = 128
            tiles = [(0, 64), (64, 64)]
            xsrc = lambda i: xs[i, :, :]
            segs = [(0, 64, 0), (64, 64, 1)]
        gi = gi_pre if gi_pre is not None else load_gain(0)
        gi_ffn = load_gain(2)
        for i, (c0, tsz) in enumerate(tiles):
            if norm_done:
                break
            if not preloaded:
                op("pool", lambda e, i=i, tsz=tsz: e.dma_start(out=xt[0:tsz, i, :], in_=xsrc(i)), writes=[t_xt[i]], chan=c_xt[i])
            norm_to_hT(xt[0:tsz, i, :], tsz, t_xt[i], gi, c0)

        gi_fin = load_gain(3)
        if prm:
            issue_cache_conv(4)
        else:
            issue_cache_conv(len(cc_list))

        def fm_group(gidx, dst_fn, dst_tok_fn):
            s = wload(("in", gidx))
            wv = wview(s, 0, KD, 512)
            for c in range(4):
                bk = nbank()
                gemm_fm(bk, wv, c * 128, 128, KD, lambda k: hT[:, k, 0:ntok], ntok, [t_ring[s], t_hT])
                copy_op(ev_eng(), dst_fn(c), ps[bk][:, 0:ntok], [t_ps[bk]], [dst_tok_fn(c)])
            return s, wv

        def tm_rows(s, wv, dst_fn, extra_fn=None, direct_fn=None):
            for i, (c0, tsz) in enumerate(tiles):
                bk = nbank()
                gemm_tm(bk, lambda k: hT[:, k, c0:c0 + tsz], tsz, wv, 0, 512, KD, [t_ring[s], t_hT])
                if direct_fn is not None:
                    direct_fn(i, bk, tsz)
                else:
                    store_rows(bk, tsz, 512, dst_fn(i), extra_fn(i, tsz) if extra_fn else None)

        def pool_copy_to(dst_ap, dst_tok):
            def extra(sap, stok):
                op("pool", lambda e: e.tensor_copy(out=dst_ap, in_=sap), reads=[stok], writes=[dst_tok])
            return extra

        if prm:
            hf = t % 2
            s, wv = fm_group(1, lambda c: kaT[:, c, hf * TS:(hf + 1) * TS], lambda c: t_kaT[hf])
            if t == NSTEP - 1:
                tm_rows(s, wv, lambda i: pa_k[b, i * 128:(i + 1) * 128, :])
            s, wv = fm_group(4, lambda c: kbT[:, c, t * TS:(t + 1) * TS], lambda c: t_kbT[t])
            tm_rows(s, wv, lambda i: pb_k[b, t * TS + i * 128:t * TS + (i + 1) * 128, :])
            s = wload(("in", 2))
            wv = wview(s, 0, KD, 512)
            if t == NSTEP - 1:
                tm_rows(s, wv, lambda i: pa_v[b, i * 128:(i + 1) * 128, :],
                        extra_fn=lambda i, tsz: pool_copy_to(va[:, (4 * t + i) % 8, 0:512], t_va[hf]))
            else:
                tm_rows(s, wv, None, direct_fn=lambda i, bk, tsz: copy_op(
                    ev_eng(), va[:, (4 * t + i) % 8, 0:512], ps[bk][:, 0:512], [t_ps[bk]], [t_va[hf]]))
            s = wload(("in", 5))
            wv = wview(s, 0, KD, 512)
            tm_rows(s, wv, lambda i: pb_v[b, t * TS + i * 128:t * TS + (i + 1) * 128, :],
                    extra_fn=lambda i, tsz: pool_copy_to(vb[:, 4 * t + i, :], t_vb[t]))
        else:
            s, wv = fm_group(1, lambda c: kaT[:, c, 0:128], lambda c: t_kaT[0])
            tm_rows(s, wv, lambda i: sa_k[i, 448:512, :])
            s, wv = fm_group(4, lambda c: kbT[:, c, 0:128], lambda c: t_kbT[0])
            tm_rows(s, wv, lambda i: sb_k[i, :, :])
            s = wload(("in", 2))
            wv = wview(s, 0, KD, 512)
            tm_rows(s, wv, lambda i: sa_v[i, 448:512, :], extra_fn=lambda i, tsz: pool_copy_to(va[0:64, i, 0:512], t_va[0]))
            s = wload(("in", 5))
            wv = wview(s, 0, KD, 512)
            tm_rows(s, wv, lambda i: sb_v[i, :, :], extra_fn=lambda i, tsz: pool_copy_to(vb[0:64, i, :], t_vb[0]))
            for i in range(2):
                op("sp", lambda e, i=i: e.dma_start(out=sa_k[i, 0:448, :], in_=cak[i, 64:512, :]), chan=c_d2d)
                op("sp", lambda e, i=i: e.dma_start(out=sa_v[i, 0:448, :], in_=cav[i, 64:512, :]), chan=c_d2d)
        ckpt(3)
        fm_group(0, lambda c: U[:, c, 0:ntok], lambda c: t_U[c])
        fm_group(3, lambda c: U[:, 4 + c, 0:ntok], lambda c: t_U[4 + c])
        fm_group(6, lambda c: U[:, 8 + c, 0:ntok], lambda c: t_U[8 + c])
        ckpt(4)
        if prm:
            grp["copies"] = []
            grp["done"] = set()
            run_pipeline(items_A_prompt(t) + items_B_prompt(t) + items_M(TS, 0))
            ckpt(7)
        else:
            for bb in range(2):
                attn_A_sample(bb)
                ckpt(30)
                attn_B_sample(bb)
                ckpt(31)
                load_mem_sample(bb)
                run_pipeline(items_M(64, bb * 64))
                ckpt(32)
        for f in range(8):
            s = wload(("mg", f))
            wva = wview(s, 0, 8, 128)
            wvb = wview(s, 1024, 4, 128)
            wvm = wview(s, 1536, 4, 128)
            wvg = wview(s, 2048, KD, 384)
            ba = nbank()
            gemm_fm(ba, wva, 0, 128, 8, lambda k: U[0:64, 12 + k, 0:ntok], ntok, [t_ring[s]] + t_U[12:20], kpart=64)
            bb_ = nbank()
            gemm_fm(bb_, wvb, 0, 128, 4, lambda k: obT[:, k, 0:ntok], ntok, [t_ring[s]] + t_obT)
            bm = nbank()
            gemm_fm(bm, wvm, 0, 128, 4, lambda k: omT[:, k, 0:ntok], ntok, [t_ring[s]] + t_omT)
            prods = []
            for bb, bo_ in enumerate((ba, bb_, bm)):
                bg_ = nbank()
                gemm_fm(bg_, wvg, bb * 128, 128, KD, lambda k: hT[:, k, 0:ntok], ntok, [t_ring[s], t_hT])
                gt, gtt = new_wk()
                op("act", lambda e, bb=bb, bg_=bg_, gt=gt: e.activation(
                    out=gt[:, 0:ntok], in_=ps[bg_][:, 0:ntok], func=AF.Sigmoid, bias=bg[:, bb * 8 + f:bb * 8 + f + 1]),
                    reads=[t_ps[bg_], t_const], writes=[gtt])
                op("dve", lambda e, bo_=bo_, gt=gt: e.tensor_tensor(
                    out=gt[:, 0:ntok], in0=ps[bo_][:, 0:ntok], in1=gt[:, 0:ntok], op=ALU.mult),
                    reads=[t_ps[bo_], gtt], writes=[gtt])
                prods.append((gt, gtt))
            (p0, p0t), (p1, p1t), (p2, p2t) = prods
            op("pool", lambda e: e.tensor_tensor(out=p0[:, 0:ntok], in0=p0[:, 0:ntok], in1=p1[:, 0:ntok], op=ALU.add),
               reads=[p0t, p1t], writes=[p0t])
            op("pool", lambda e: e.tensor_tensor(out=U[:, f, 0:ntok], in0=p0[:, 0:ntok], in1=p2[:, 0:ntok], op=ALU.add),
               reads=[p0t, p2t], writes=[t_U[f]])
        ckpt(8)
        so = [wload(("out", cg)) for cg in range(2)]
        for i, (c0, tsz) in enumerate(tiles):
            for cg in range(2):
                wv = wview(so[cg], 0, KD, 512)
                bk = nbank()
                gemm_tm(bk, lambda k: U[:, k, c0:c0 + tsz], tsz, wv, 0, 512, KD, [t_ring[so[cg]]], ktoks=t_U)
                op("dve", lambda e, i=i, tsz=tsz, bk=bk, cg=cg: e.tensor_tensor(
                    out=xt[0:tsz, i, cg * 512:(cg + 1) * 512], in0=ps[bk][0:tsz, 0:512],
                    in1=xt[0:tsz, i, cg * 512:(cg + 1) * 512], op=ALU.add), reads=[t_ps[bk], t_xt[i]], writes=[t_xt[i]])
        ckpt(9)
        for i, (c0, tsz) in enumerate(tiles):
            norm_to_hT(xt[0:tsz, i, :], tsz, t_xt[i], gi_ffn, c0)
        gi_next = load_gain(0) if has_next else None
        for cp in range(NFF // 2):
            s = wload(("up", cp))
            wa = wview(s, 0, KD, 256)
            wg = wview(s, 2048, KD, 256)
            for cc in range(2):
                c = cp * 2 + cc
                bka = nbank()
                gemm_fm(bka, wa, cc * 128, 128, KD, lambda k: hT[:, k, 0:ntok], ntok, [t_ring[s], t_hT])
                bkg = nbank()
                gemm_fm(bkg, wg, cc * 128, 128, KD, lambda k: hT[:, k, 0:ntok], ntok, [t_ring[s], t_hT])
                ai = nxt("abuf", 3)
                cv, cvt = new_wk()
                for (q0, n, hb) in segs:
                    a0 = q0 + 2 * hb
                    op("pool", lambda e, a0=a0, hb=hb: e.tensor_copy(out=abuf[:, ai, a0:a0 + 2], in_=ahist[:, hb, c, :]),
                       reads=[t_ahist], writes=[t_abuf[ai]])
                    op("act", lambda e, a0=a0, q0=q0, n=n: e.activation(out=abuf[:, ai, a0 + 2:a0 + 2 + n], in_=ps[bka][:, q0:q0 + n],
                                                                        func=AF.Copy),
                       reads=[t_ps[bka]], writes=[t_abuf[ai]])
                    op("pool", lambda e, a0=a0, n=n, hb=hb: e.tensor_copy(out=ahist[:, hb, c, :], in_=abuf[:, ai, a0 + n:a0 + n + 2]),
                       reads=[t_abuf[ai]], writes=[t_ahist])
                    op("dve", lambda e, a0=a0, q0=q0, n=n: e.tensor_scalar(
                        out=cv[:, q0:q0 + n], in0=abuf[:, ai, a0 + 2:a0 + 2 + n], scalar1=wc[:, c, 2:3], scalar2=bc[:, c:c + 1],
                        op0=ALU.mult, op1=ALU.add), reads=[t_abuf[ai], t_const], writes=[cvt])
                    op("dve", lambda e, a0=a0, q0=q0, n=n: e.scalar_tensor_tensor(
                        out=cv[:, q0:q0 + n], in0=abuf[:, ai, a0 + 1:a0 + 1 + n], scalar=wc[:, c, 1:2], in1=cv[:, q0:q0 + n],
                        op0=ALU.mult, op1=ALU.add), reads=[t_abuf[ai], t_const, cvt], writes=[cvt])
                    op("dve", lambda e, a0=a0, q0=q0, n=n: e.scalar_tensor_tensor(
                        out=cv[:, q0:q0 + n], in0=abuf[:, ai, a0:a0 + n], scalar=wc[:, c, 0:1], in1=cv[:, q0:q0 + n],
                        op0=ALU.mult, op1=ALU.add), reads=[t_abuf[ai], t_const, cvt], writes=[cvt])
                op("act", lambda e: e.activation(out=cv[:, 0:ntok], in_=cv[:, 0:ntok], func=AF.Gelu_apprx_tanh),
                   reads=[cvt], writes=[cvt])
                op("dve", lambda e: e.tensor_tensor(out=U[:, c, 0:ntok], in0=ps[bkg][:, 0:ntok], in1=cv[:, 0:ntok], op=ALU.mult),
                   reads=[t_ps[bkg], cvt], writes=[t_U[c]])
        ckpt(10)
        if next_x is not None:
            for i, (ntsz, nsrc) in enumerate(next_x):
                op("pool", lambda e, i=i, ntsz=ntsz, nsrc=nsrc: e.dma_start(out=xal[i][0:ntsz, :], in_=nsrc),
                   writes=xal_tok[i], chan=c_xal[i])
        if (prm and t == NSTEP - 1) or not prm:
            for hb in ([0] if prm else [0, 1]):
                bk = nbank()
                for j in range(2):
                    op("pe", lambda e, j=j: e.transpose(out=ps[bk][0:NFF, j * 128:(j + 1) * 128], in_=ahist[:, hb, :, j],
                                                        identity=identf[:, :]),
                       reads=[t_ahist, t_const], writes=[t_ps[bk]])
                dst = (pconv[b] if prm else sconv[hb]).rearrange("j (c p) -> c j p", p=128)
                i_ = nxt("stg", 3)
                copy_op(ev_eng(), stg[0:NFF, i_, 0:256], ps[bk][0:NFF, 0:256], [t_ps[bk]], [t_stg[i_]])
                op("act", lambda e, i_=i_: e.dma_start(out=dst, in_=stg[0:NFF, i_, 0:256].rearrange("c (j p) -> c j p", j=2)),
                   reads=[t_stg[i_]], chan=c_stg[i_])
        pre_j = []
        if next_x is not None:
            for i, (ntsz, nsrc) in enumerate(next_x[:2]):
                pre_j.append(norm_prep(xal[i][0:ntsz, :], ntsz, xal_tok[i], gi_next))
        for cg in range(4):
            s = wload(("dn", cg))
            wv = wview(s, 0, NFF, 256)
            for i, (c0, tsz) in enumerate(tiles):
                bk = nbank()
                gemm_tm(bk, lambda k: U[:, k, c0:c0 + tsz], tsz, wv, 0, 256, NFF, [t_ring[s]], ktoks=t_U)
                op("dve", lambda e, i=i, tsz=tsz, bk=bk: e.tensor_tensor(
                    out=xt[0:tsz, i, cg * 256:(cg + 1) * 256], in0=ps[bk][0:tsz, 0:256],
                    in1=xt[0:tsz, i, cg * 256:(cg + 1) * 256], op=ALU.add), reads=[t_ps[bk], t_xt[i]], writes=[t_xt[i]])
        ckpt(11)
        if next_x is not None:
            noff = 0
            for i, (ntsz, nsrc) in enumerate(next_x):
                if i < len(pre_j):
                    norm_xpose(pre_j[i], ntsz, noff)
                else:
                    norm_to_hT(xal[i][0:ntsz, :], ntsz, xal_tok[i], gi_next, noff)
                noff += ntsz
        gi = gi_fin
        for i, (c0, tsz) in enumerate(tiles):
            rstd, rt = rms_stats(xt[0:tsz, i, :], tsz, t_xt[i])
            op("dve", lambda e, i=i, tsz=tsz: e.scalar_tensor_tensor(
                out=xt[0:tsz, i, :], in0=xt[0:tsz, i, :], scalar=rstd[0:tsz, :], in1=gb[0:tsz, gi, :],
                op0=ALU.mult, op1=ALU.mult), reads=[t_xt[i], rt, t_gb[gi]], writes=[t_xt[i]])
            dst = y_p[b, t * TS + c0:t * TS + c0 + tsz, :] if prm else y_s[i, :, :]
            op("act", lambda e, i=i, tsz=tsz, dst=dst: e.dma_start(out=dst, in_=xt[0:tsz, i, :]), reads=[t_xt[i]], chan=c_yst[i])
            if next_x is not None and i < len(next_x):
                ntsz, nsrc = next_x[i]
                op("pool", lambda e, i=i, ntsz=ntsz: e.tensor_copy(out=xt[0:ntsz, i, :], in_=xal[i][0:ntsz, :]),
                   reads=xal_tok[i], writes=[t_xt[i]])
        return gi_next

    ckpt(1)
    gi_carry = None
    for b in range(2):
        if ONLY[0] == "s":
            break
        mem_kv(b)
        if b == 0:
            late_constants()
        ckpt(2)
        op("pool", lambda e: e.memset(ahist[:, 0, :, :], 0.0), writes=[t_ahist])
        for t in range(NSTEP):
            if t < NSTEP - 1:
                nx = [(128, xp[b, (t + 1) * TS + i * 128:(t + 1) * TS + (i + 1) * 128, :]) for i in range(4)]
            elif b == 1:
                nx = [(64, xs[i, :, :]) for i in range(2)]
            else:
                nx = None
            if t == 0:
                gi_carry = None
            gi_carry = step("p", b, t, preloaded=(t > 0), next_x=nx, gi_pre=gi_carry,
                            has_next=(t < NSTEP - 1 or b == 1), norm_done=(t > 0))
            ckpt(12 + t)
        ckpt(20)
    op("sp", lambda e: e.dma_start(out=ahist[:, :, :, :], in_=sconvT[:, :, :, :]), writes=[t_ahist], chan=c_misc)
    step("s", 0, 0, preloaded=(ONLY[0] != "s"), gi_pre=(gi_carry if ONLY[0] != "s" else None), norm_done=(ONLY[0] != "s"))


_CACHE = {}


def kernel(x_prompt, x_sample, mem_prompt, cache_a_k, cache_a_v, cache_b_k, cache_b_v,
           cache_mem_k, cache_mem_v, state_ffn_conv, g_mix, w_in, b_gate, rel_bias,
           lam_q, lam_k, g_sub, g_mem, w_mem_kv, w_oa, w_ob, w_om, w_out, g_ffn,
           w_up, w_conv, b_conv, w_down, g_final):
    f = lambda a: np.ascontiguousarray(np.asarray(a, dtype=np.float32))
    if "nc" not in _CACHE:
        _CACHE["nc"] = build_nc()
    nc, _es = _CACHE["nc"]
    g4 = f(np.stack([np.tile(np.asarray(g).reshape(1, D), (128, 1)) for g in (g_mix[0], g_mem[0], g_ffn[0], g_final)]))
    bgT = f(np.asarray(b_gate[0]).reshape(24, 128).T)
    wcT = f(np.asarray(w_conv[0]).reshape(3, NFF, 128).transpose(2, 1, 0))
    bcT = f(np.asarray(b_conv[0]).reshape(NFF, 128).T)
    gsub = f(np.asarray(g_sub[0]).reshape(128, 1))
    lamq = f(np.tile(np.asarray(lam_q[0]).reshape(1, 128), (128, 1)))
    lamk = f(np.tile(np.asarray(lam_k[0]).reshape(1, 128), (128, 1)))
    p_ = np.arange(128)[:, None]
    x_ = np.arange(640)[None, :]
    idx = np.clip(x_ - p_, -128, 128) + 128
    rb = np.asarray(rel_bias[0])
    relT = f(rb[:, idx].transpose(1, 0, 2))
    rbfar = f(np.tile(rb[:, 256].reshape(1, 8), (128, 1)))
    ident = np.eye(128, dtype=np.float32)
    kmq = f(p_ - np.arange(128)[None, :])
    posB = f(128.0 * (np.arange(16)[None, :] - 12) + p_ - 256.0)
    posS = f(128.0 * (np.arange(33)[None, :] - 32) + p_)
    shared = dict(w_in=f(w_in[0]), w_mem=f(w_mem_kv[0]), w_oa=f(w_oa[0]), w_ob=f(w_ob[0]), w_om=f(w_om[0]),
                  w_out=f(w_out[0]), w_up=f(w_up[0]), w_down=f(w_down[0]), g4=g4, bgT=bgT, wcT=wcT, bcT=bcT,
                  gsub=gsub, lamq=lamq, lamk=lamk, relT=relT, rbfar=rbfar, ident=ident, kmq=kmq, posB=posB, posS=posS)
    in_maps = []
    for c in range(8):
        sl = slice(2 * c, 2 * c + 2)
        m = dict(shared)
        m.update(xp=f(x_prompt[sl]), xs=f(x_sample[sl]), memp=f(mem_prompt[sl]),
                 cak=f(np.asarray(cache_a_k[0, sl]).reshape(2, 512, 512)), cav=f(np.asarray(cache_a_v[0, sl]).reshape(2, 512, 512)),
                 cbk=f(np.asarray(cache_b_k[0, sl]).reshape(2, LPAST, 512)), cbv=f(np.asarray(cache_b_v[0, sl]).reshape(2, LPAST, 512)),
                 cmk=f(np.asarray(cache_mem_k[0, sl]).reshape(2, 256, 512)), cmv=f(np.asarray(cache_mem_v[0, sl]).reshape(2, 256, 512)),
                 sconvT=f(np.asarray(state_ffn_conv[0, sl]).reshape(2, 2, NFF, 128).transpose(3, 0, 2, 1)))
        in_maps.append(m)
    res = run_bass_kernel_spmd(nc, in_maps, core_ids=list(range(8)))
    R = res.results
    cat = lambda k: np.concatenate([np.asarray(r[k], dtype=np.float32) for r in R], axis=0)
    y_p = cat("y_p"); y_s = cat("y_s")
    outs = (y_p, y_s,
            cat("pa_k").reshape(1, 16, 512, 8, 64), cat("pa_v").reshape(1, 16, 512, 8, 64),
            cat("pb_k").reshape(1, 16, SEQ, 4, 128), cat("pb_v").reshape(1, 16, SEQ, 4, 128),
            cat("pm_k").reshape(1, 16, 256, 4, 128), cat("pm_v").reshape(1, 16, 256, 4, 128),
            cat("pconv").reshape(1, 16, 2, DFF),
            cat("sa_k").reshape(1, 16, 512, 8, 64), cat("sa_v").reshape(1, 16, 512, 8, 64),
            cat("sb_k").reshape(1, 16, 64, 4, 128), cat("sb_v").reshape(1, 16, 64, 4, 128),
            cat("sconv").reshape(1, 16, 2, DFF))
    return outs
```

```python
import numpy as np
from contextlib import ExitStack
import concourse.bass as bass
import concourse.mybir as mybir
from concourse.bass_utils import run_bass_kernel_spmd

F32 = mybir.dt.float32
BF16 = mybir.dt.bfloat16
AF = mybir.ActivationFunctionType
ALU = mybir.AluOpType
AX = mybir.AxisListType


class Tok:
    __slots__ = ("name", "w", "r")

    def __init__(self, name):
        self.name = name
        self.w = None
        self.r = {}


class Chan:
    __slots__ = ("name", "sem", "cnt", "last")

    def __init__(self, name):
        self.name = name
        self.sem = None
        self.cnt = 0
        self.last = None


class Op:
    __slots__ = ("eng", "fn", "deps", "marked", "chan", "sig", "n_dma", "lazy")


class Sched:
    def __init__(self, nc, es, marks=None):
        self.nc = nc
        self.es = es
        self.marks = marks
        self.E = {"pe": nc.tensor, "act": nc.scalar, "dve": nc.vector,
                  "pool": nc.gpsimd, "sp": nc.sync}
        self.ops = []
        self.chans = []
        self.n = 0
        if marks is not None:
            self.esem = {e: es.enter_context(nc.semaphore("se_" + e)) for e in self.E}
            self.cnt = {e: 0 for e in self.E}
            self.waited = {e: {} for e in self.E}

    def chan(self, name):
        c = Chan(name)
        self.chans.append(c)
        return c

    def op(self, eng, fn, reads=(), writes=(), chan=None, n_dma=1, extra_deps=(), serialize=True, lazy=False):
        o = Op()
        o.lazy = lazy
        o.eng = eng
        o.fn = None
        o.chan = chan
        o.deps = []
        o.marked = False
        o.sig = None
        o.n_dma = n_dma
        seen = set()
        is_dma = chan is not None

        def add(d, raw):
            if d is None or id(d) in seen:
                return
            if d.chan is None and not is_dma and d.eng == eng:
                if eng == "pe" or not raw:
                    return
            seen.add(id(d))
            o.deps.append(d)

        for t in reads:
            add(t.w, True)
        for t in writes:
            add(t.w, False)
            for r in t.r.values():
                add(r, False)
        for d in extra_deps:
            add(d, True)
        if is_dma and serialize:
            add(chan.last, True)
            chan.last = o
        for t in reads:
            t.r[id(chan) if is_dma else eng] = o
        for t in writes:
            t.w = o
            t.r = {}
        idx = self.n
        self.n += 1
        if self.marks is None:
            for d in o.deps:
                d.marked = True
            self.ops.append(o)
            o.deps = []
            return o
        e = self.E[eng]
        w = self.waited[eng]
        for d in o.deps:
            if d.lazy:
                key, sem, val = id(d.chan), d.chan.sem, d.chan.cnt
            else:
                key, sem, val = d.sig
            if w.get(key, 0) >= val:
                continue
            e.wait_ge(sem, val)
            w[key] = val
        o.deps = []
        ins = fn(e)
        if chan is not None:
            c = chan
            if c.sem is None:
                c.sem = self.es.enter_context(self.nc.semaphore("sc_" + c.name))
            lst = ins if isinstance(ins, (list, tuple)) else [ins]
            for i_ in lst:
                i_.then_inc(c.sem, 16)
                c.cnt += 16
            o.sig = (id(c), c.sem, c.cnt)
        elif self.marks[idx]:
            self.cnt[eng] += 1
            ins.then_inc(self.esem[eng], 1)
            o.sig = (eng, self.esem[eng], self.cnt[eng])
        return o

    def get_marks(self):
        return [o.marked for o in self.ops]

    def finish(self, final_wait_eng="sp"):
        if self.marks is None:
            return
        e = self.E[final_wait_eng]
        for c in self.chans:
            if c.sem is not None and c.cnt > 0:
                e.wait_ge(c.sem, c.cnt)


D = 1024
KD = 8
SEQ = 2048
TS = 512
NSTEP = SEQ // TS
DFF = 2816
NFF = 22
LPAST = 4096
EPS = 1e-6
LAM_INIT = 0.2
SLOPES = [2.0 ** (-8.0 * (h + 1) / 4.0) for h in range(4)]
RING_COLS = 5632
NRING = 3


def build_nc():
    nc1 = bass.Bass("TRN2", target_bir_lowering=False)
    with ExitStack() as es1:
        S1 = Sched(nc1, es1, None)
        build_into(nc1, es1, S1)
        marks = S1.get_marks()
    nc = bass.Bass("TRN2", target_bir_lowering=False)
    es = ExitStack()
    S = Sched(nc, es, marks)
    build_into(nc, es, S)
    assert S.n == len(marks)
    S.finish()
    return nc, es


STOP = [None]
ONLY = [None]


class _Stop(Exception):
    pass


def build_into(nc, es, S):
    try:
        _build_into(nc, es, S)
    except _Stop:
        pass


def _build_into(nc, es, S):
    op = S.op

    def ckpt(n):
        if STOP[0] is not None and STOP[0] == n:
            raise _Stop()

    def din(name, shape):
        return nc.dram_tensor(name, list(shape), F32, kind="ExternalInput").ap()

    def dout(name, shape):
        return nc.dram_tensor(name, list(shape), F32, kind="ExternalOutput").ap()

    def sb(name, shape, dt):
        return es.enter_context(nc.sbuf_tensor("s_" + name, list(shape), dt))

    xp = din("xp", [2, SEQ, D]); xs = din("xs", [2, 64, D]); memp = din("memp", [2, 256, D])
    cak = din("cak", [2, 512, 512]); cav = din("cav", [2, 512, 512])
    cbk = din("cbk", [2, LPAST, 512]); cbv = din("cbv", [2, LPAST, 512])
    cmk = din("cmk", [2, 256, 512]); cmv = din("cmv", [2, 256, 512])
    sconvT = din("sconvT", [128, 2, NFF, 2])
    w_in = din("w_in", [D, 6656]); w_mem = din("w_mem", [D, 1024])
    w_oa = din("w_oa", [512, D]); w_ob = din("w_ob", [512, D]); w_om = din("w_om", [512, D])
    w_out = din("w_out", [D, D]); w_up = din("w_up", [D, 2 * DFF]); w_down = din("w_down", [DFF, D])
    g4 = din("g4", [4, 128, D])
    bgT = din("bgT", [128, 24]); wcT = din("wcT", [128, NFF, 3]); bcT = din("bcT", [128, NFF])
    gsub = din("gsub", [128, 1]); lamq = din("lamq", [128, 128]); lamk = din("lamk", [128, 128])
    relT_d = din("relT", [128, 8, 640]); rbfar_d = din("rbfar", [128, 8])
    ident_d = din("ident", [128, 128]); kmq_d = din("kmq", [128, 128])
    posB_d = din("posB", [128, 16]); posS_d = din("posS", [128, 33])

    y_p = dout("y_p", [2, SEQ, D]); y_s = dout("y_s", [2, 64, D])
    pa_k = dout("pa_k", [2, 512, 512]); pa_v = dout("pa_v", [2, 512, 512])
    pb_k = dout("pb_k", [2, SEQ, 512]); pb_v = dout("pb_v", [2, SEQ, 512])
    pm_k = dout("pm_k", [2, 256, 512]); pm_v = dout("pm_v", [2, 256, 512])
    pconv = dout("pconv", [2, 2, DFF])
    sa_k = dout("sa_k", [2, 512, 512]); sa_v = dout("sa_v", [2, 512, 512])
    sb_k = dout("sb_k", [2, 64, 512]); sb_v = dout("sb_v", [2, 64, 512])
    sconv = dout("sconv", [2, 2, DFF])

    ident = sb("ident", [128, 128], BF16); ones = sb("ones", [128, 128], BF16)
    identf = sb("identf", [128, 128], F32)
    onesf = sb("onesf", [128, 128], F32)
    strip = sb("strip", [128, 8, 640], BF16)
    mcorr = sb("mcorr", [128, 4, 128], BF16)
    biasB = sb("biasB", [128, 4, 16], F32); biasS = sb("biasS", [128, 4, 33], F32)
    biasB0 = sb("biasB0", [128, 2, 16], F32)
    rbfar = sb("rbfar", [128, 8], F32)
    bg = sb("bg", [128, 24], F32); wc = sb("wc", [128, NFF, 3], F32); bc = sb("bc", [128, NFF], F32)
    gsubs = sb("gsubs", [128, 1], F32); nlam = sb("nlam", [128, 1], F32)
    gb = sb("gb", [128, 2, D], F32)
    kbT = sb("kbT", [128, 4, SEQ], BF16); vb = sb("vb", [128, 16, 512], BF16)
    kaT = sb("kaT", [128, 4, 1024], BF16); va = sb("va", [128, 8, 576], BF16)
    mkT = sb("mkT", [128, 4, 256], BF16); mv = sb("mv", [128, 2, 512], BF16)
    xt = sb("xt", [128, 4, D], F32)
    hT = sb("hT", [128, KD, TS], BF16)
    U = sb("U", [128, NFF, TS], BF16)
    obT = sb("obT", [128, 4, TS], BF16); omT = sb("omT", [128, 4, TS], BF16)
    ring = sb("ring", [128, NRING, RING_COLS], BF16)
    pT = sb("pT", [128, 4, 512], BF16)
    qzs = sb("qzs", [128, 8, 64], BF16)
    qz = sb("qz", [128, 2, 512], BF16)
    junk = pT[:, 0:2, :].rearrange("p a b -> p (a b)")
    wk = sb("wk", [128, 8, 512], F32)
    abuf = sb("abuf", [128, 3, 514], F32)
    ahist = sb("ahist", [128, 2, NFF, 2], F32)
    stg = sb("stg", [128, 3, 512], F32)
    xn = sb("xn", [128, 2, D], BF16)
    stat = sb("stat", [128, 64], F32)
    ktile = sb("ktile", [128, 2, 512], BF16)
    kTt = sb("kTt", [128, 2, 512], BF16)
    vtile = sb("vtile", [128, 2, 512], BF16)
    ps = [es.enter_context(nc.psum_tensor("ps%d" % i, [128, 512], F32)) for i in range(8)]
    psT = ps[7][:].bitcast(BF16)

    def toks(name, n):
        return [Tok("%s%d" % (name, i)) for i in range(n)]
    t_const = Tok("const")
    t_gb = toks("gb", 2); t_kbT = toks("kbT", 4); t_vb = toks("vb", 4); t_kaT = toks("kaT", 2); t_va = toks("va", 2)
    t_mkT = Tok("mkT"); t_mv = Tok("mv"); t_xt = toks("xt", 4); t_hT = Tok("hT"); t_U = toks("U", NFF)
    t_obT = toks("obT", 4); t_omT = toks("omT", 4); t_ring = toks("ring", NRING); t_pT = toks("pT", 4)
    t_wk = toks("wk", 8); t_abuf = toks("abuf", 3); t_ahist = Tok("ahist"); t_stg = toks("stg", 3)
    t_xn = toks("xn", 2); t_junk = Tok("junk"); t_stat = toks("stat", 64)
    t_ktile = toks("ktile", 2); t_kTt = toks("kTt", 2); t_vtile = toks("vtile", 2)
    t_ps = toks("ps", 8)
    t_qz = toks("qz", 2)
    t_qzs = Tok("qzs")
    c_ring = [S.chan("ring%d" % i) for i in range(NRING)]
    c_xt = [S.chan("xt%d" % i) for i in range(4)]
    c_stg = [S.chan("stg%d" % i) for i in range(3)]
    c_yst = [S.chan("yst%d" % i) for i in range(4)]; c_gb = [S.chan("gb%d" % i) for i in range(2)]
    c_const = S.chan("const"); c_misc = S.chan("misc"); c_constp = S.chan("constp"); c_miscp = S.chan("miscp")
    c_ktile = [S.chan("ktile%d" % i) for i in range(2)]; c_vtile = [S.chan("vtile%d" % i) for i in range(2)]
    c_d2d = S.chan("d2d")
    c_ktile2 = [S.chan("ktileh%d" % i) for i in range(2)]; c_vtile2 = [S.chan("vtileh%d" % i) for i in range(2)]

    rr = {"ring": 0, "wk": 0, "stat": 0, "pT": 0, "stg": 0, "xn": 0, "gb": 0, "abuf": 0, "ev": 0,
          "kt": 0, "vt": 0}

    def nxt(name, n):
        i = rr[name] % n
        rr[name] += 1
        return i

    def ev_eng():
        return "act" if nxt("ev", 2) == 0 else "dve"

    def copy_op(eng, out, in_, reads, writes):
        if eng == "act":
            return op("act", lambda e: e.activation(out=out, in_=in_, func=AF.Copy), reads=reads, writes=writes)
        return op(eng, lambda e: e.tensor_copy(out=out, in_=in_), reads=reads, writes=writes)

    def new_stat():
        i = nxt("stat", 64)
        return stat[:, i:i + 1], t_stat[i]

    def new_wk():
        i = nxt("wk", 8)
        return wk[:, i, :], t_wk[i]

    def w_cols(W, K, c0, ncol):
        return W.rearrange("(k p) c -> p k c", p=128)[:, :, c0:c0 + ncol]

    w_oa_v = w_oa.rearrange("(h p) c -> p h c", p=64)
    w_down_v = w_down.rearrange("(k p) c -> p k c", p=128)

    def slot_parts(key):
        kind, i = key
        if kind == "mem":
            return [(0, KD, 512, w_cols(w_mem, KD, i * 512, 512))]
        if kind == "in":
            return [(0, KD, 512, w_cols(w_in, KD, i * 512, 512))]
        if kind == "mg":
            f = i
            parts = [(0, 8, 128, w_oa_v[:, :, f * 128:(f + 1) * 128], 64),
                     (1024, 4, 128, w_cols(w_ob, 4, f * 128, 128)),
                     (1536, 4, 128, w_cols(w_om, 4, f * 128, 128))]
            for bb in range(3):
                cg = 3584 + bb * 1024 + f * 128
                parts.append((2048 + bb * 128, KD, 128, w_cols(w_in, KD, cg, 128), 128, 384))
            return parts
        if kind == "out":
            return [(0, KD, 512, w_cols(w_out, KD, i * 512, 512))]
        if kind == "up":
            return [(0, KD, 256, w_cols(w_up, KD, i * 256, 256)),
                    (2048, KD, 256, w_cols(w_up, KD, DFF + i * 256, 256))]
        if kind == "dn":
            return [(0, NFF, 256, w_down_v[:, :, i * 256:(i + 1) * 256])]
        raise KeyError(key)

    slot_keys = ([("mem", 0), ("mem", 1)] + [("in", g) for g in (1, 4, 2, 5, 0, 3, 6)] +
                 [("mg", f) for f in range(8)] + [("out", 0), ("out", 1)] +
                 [("up", i) for i in range(NFF // 2)] + [("dn", i) for i in range(4)])
    groups = {"mem": 0, "in": 1, "mg": 2, "out": 2, "up": 3, "dn": 4}
    wscr = nc.dram_tensor("wscr", [len(slot_keys), 128, RING_COLS], BF16, kind="Internal").ap()
    NCONV = 5
    AHEAD = 8
    c_conv = [S.chan("conv%d" % i) for i in range(NCONV)]
    slot_info = {}
    slot_index = {key: i for i, key in enumerate(slot_keys)}
    conv_state = {"next": 0}

    def issue_conv(upto):
        while conv_state["next"] <= min(upto, len(slot_keys) - 1):
            idx = conv_state["next"]
            conv_state["next"] += 1
            key = slot_keys[idx]
            parts = slot_parts(key)
            ext = 0
            for prt in parts:
                c0, K, ncol = prt[0], prt[1], prt[2]
                cs = prt[5] if len(prt) > 5 else ncol
                ext = max(ext, c0 + (K - 1) * cs + ncol)

            def fn(e, idx=idx, parts=parts):
                lst = []
                for prt in parts:
                    c0, K, ncol, src = prt[:4]
                    np_ = prt[4] if len(prt) > 4 else 128
                    cs = prt[5] if len(prt) > 5 else ncol
                    dst = wscr[idx, 0:np_, c0:c0 + K * cs].rearrange("p (k c) -> p k c", k=K)[:, :, 0:ncol]
                    lst.append(e.dma_start(out=dst, in_=src))
                return lst
            o = op("pool", fn, chan=c_conv[idx % NCONV])
            slot_info[key] = (idx, ext, o)

    issue_conv(AHEAD - 1)

    def wload(key):
        issue_conv(slot_index[key] + AHEAD)
        idx, ext, conv_op = slot_info[key]
        s = nxt("ring", NRING)
        if key[0] == "mg":
            regs = [(0, 64, 0, ext), (64, 128, 1024, ext)]
        else:
            regs = [(0, 128, 0, ext)]
        op("sp", lambda e: [e.dma_start(out=ring[p0:p1, s, c0:c1], in_=wscr[idx, p0:p1, c0:c1]) for (p0, p1, c0, c1) in regs],
           writes=[t_ring[s]], chan=c_ring[s], extra_deps=[conv_op])
        return s

    cb16 = nc.dram_tensor("cb16", [2, 2, LPAST, 512], BF16, kind="Internal").ap()
    c_cc = [S.chan("cc%d" % i) for i in range(4)]
    cc_ops = {}
    cc_list = [(kv, bb, ch) for bb in range(2) for ch in range(8) for kv in range(2)]
    cc_state = {"next": 0}

    def issue_cache_conv(n):
        for _ in range(n):
            if cc_state["next"] >= len(cc_list):
                return
            j = cc_state["next"]
            cc_state["next"] += 1
            kv, bb, ch = cc_list[j]
            src = (cbk if kv == 0 else cbv)[bb, ch * 512:(ch + 1) * 512, :]
            o = op("pool", lambda e, kv=kv, bb=bb, ch=ch, src=src: e.dma_start(out=cb16[kv, bb, ch * 512:(ch + 1) * 512, :], in_=src),
                   chan=c_cc[j % 4])
            cc_ops[(kv, bb, ch)] = o

    def wview(s, c0, K, ncol):
        return ring[:, s, c0:c0 + K * ncol].rearrange("p (k c) -> p k c", k=K)

    tmp_toks = []

    def cload(dst, src, queue="sp"):
        tok = Tok("c%d" % len(tmp_toks))
        ch = S.chan("cst%d" % len(tmp_toks))
        op(queue, lambda e: e.dma_start(out=dst, in_=src), writes=[tok], chan=ch)
        tmp_toks.append(tok)
        return tok

    cload(ident[:], ident_d[:, :], queue="pool")
    cload(identf[:], ident_d[:, :])
    cload(rbfar[:], rbfar_d[:, :])
    cload(bg[:], bgT[:, :]); cload(wc[:], wcT[:, :, :]); cload(bc[:], bcT[:, :])
    cload(gsubs[:], gsub[:, :])
    lq_w, lq_t = new_wk(); lk_w, lk_t = new_wk(); pos_w, pos_t = new_wk(); kq_w, kq_t = new_wk()
    t1 = cload(lq_w[:, 0:128], lamq[:, :]); t2 = cload(lk_w[:, 0:128], lamk[:, :])
    t3 = cload(pos_w[:, 0:16], posB_d[:, :]); t4 = cload(pos_w[:, 16:49], posS_d[:, :])
    t5 = cload(kq_w[:, 0:128], kmq_d[:, :])
    Uf = U[:].rearrange("p a b -> p (a b)").bitcast(F32)
    t6 = cload(Uf[:, 0:5120], relT_d.rearrange("p h x -> p (h x)"))
    op("dve", lambda e: e.memset(ones[:], 1.0), reads=tmp_toks, writes=[t_const, lq_t, lk_t, pos_t, kq_t] + t_U)
    op("pool", lambda e: e.memset(va[:, :, 512:576], 0.0), writes=[t_va[0], t_va[1]])
    op("pool", lambda e: e.memset(qz[:, :, :], 0.0), writes=[t_qz[0], t_qz[1]])
    op("pool", lambda e: e.memset(qzs[:, :, :], 0.0), writes=[t_qzs])
    op("dve", lambda e: e.memset(onesf[:], 1.0), writes=[t_const])

    def late_constants():
        op("dve", lambda e: e.tensor_scalar(out=gsubs[:], in0=gsubs[:], scalar1=1.0 - LAM_INIT, scalar2=None, op0=ALU.mult),
           reads=[t_const], writes=[t_const])
        op("dve", lambda e: e.tensor_tensor(out=lq_w[:, 0:128], in0=lq_w[:, 0:128], in1=lk_w[:, 0:128], op=ALU.mult),
           reads=[lq_t, lk_t], writes=[lq_t])
        s1, s1t = new_stat(); s2, s2t = new_stat()
        op("dve", lambda e: e.tensor_reduce(out=s1, in_=lq_w[:, 0:64], axis=AX.X, op=ALU.add), reads=[lq_t], writes=[s1t])
        op("dve", lambda e: e.tensor_reduce(out=s2, in_=lq_w[:, 64:128], axis=AX.X, op=ALU.add), reads=[lq_t], writes=[s2t])
        op("act", lambda e: e.activation(out=s1, in_=s1, func=AF.Exp), reads=[s1t], writes=[s1t])
        op("act", lambda e: e.activation(out=s2, in_=s2, func=AF.Exp), reads=[s2t], writes=[s2t])
        op("dve", lambda e: e.tensor_tensor(out=nlam[:], in0=s2, in1=s1, op=ALU.subtract), reads=[s1t, s2t], writes=[t_const])
        op("dve", lambda e: e.tensor_scalar(out=nlam[:], in0=nlam[:], scalar1=-LAM_INIT, scalar2=None, op0=ALU.add),
           reads=[t_const], writes=[t_const])
        for h in range(4):
            op("dve", lambda e, h=h: e.tensor_scalar(out=biasB[:, h, :], in0=pos_w[:, 0:16], scalar1=SLOPES[h], scalar2=None, op0=ALU.mult),
               reads=[pos_t], writes=[t_const])
            op("dve", lambda e, h=h: e.tensor_scalar(out=biasS[:, h, :], in0=pos_w[:, 16:49], scalar1=SLOPES[h], scalar2=None, op0=ALU.mult),
               reads=[pos_t], writes=[t_const])
        for hf_ in range(2):
            op("dve", lambda e, hf_=hf_: e.tensor_scalar(out=biasB0[:, hf_, :], in0=pos_w[:, 0:16], scalar1=SLOPES[0],
                                                         scalar2=SLOPES[0] * (128.0 if hf_ == 0 else -128.0), op0=ALU.mult, op1=ALU.add),
               reads=[pos_t], writes=[t_const])
        op("dve", lambda e: e.tensor_scalar(out=kq_w[:, 0:128], in0=kq_w[:, 0:128], scalar1=0.0, scalar2=None, op0=ALU.max),
           reads=[kq_t], writes=[kq_t])
        for h in range(4):
            op("act", lambda e, h=h: e.activation(out=mcorr[:, h, :], in_=kq_w[:, 0:128], func=AF.Exp, scale=-2.0 * SLOPES[h]),
               reads=[kq_t], writes=[t_const])
            op("dve", lambda e, h=h: e.memset(mcorr[64:128, h, 0:64], 0.0), reads=[t_const], writes=[t_const])
        nrb, nrbt = new_wk()
        op("dve", lambda e: e.tensor_scalar(out=nrb[:, 0:8], in0=rbfar[:], scalar1=-1.0, scalar2=None, op0=ALU.mult),
           reads=[t_const], writes=[nrbt])
        for h in range(8):
            op("act", lambda e, h=h: e.activation(out=strip[:, h, :], in_=Uf[:, h * 640:(h + 1) * 640], func=AF.Exp, bias=nrb[:, h:h + 1]),
               reads=t_U + [nrbt], writes=[t_const])

    xal = [obT[:].rearrange("p a b -> p (a b)").bitcast(F32), omT[:].rearrange("p a b -> p (a b)").bitcast(F32),
           stg[:, 0:2, :].rearrange("p a b -> p (a b)"), abuf[:].rearrange("p a b -> p (a b)")[:, 0:1024]]
    xal_tok = [t_obT, t_omT, [t_stg[0], t_stg[1]], t_abuf]
    c_xal = [S.chan("xal%d" % i) for i in range(4)]

    def load_gain(which):
        i = nxt("gb", 2)
        op("pool", lambda e: e.dma_start(out=gb[:, i, :], in_=g4[which, :, :]), writes=[t_gb[i]], chan=c_gb[i])
        return i

    def rms_stats(x_ap, tsz, xtok):
        xtoks = xtok if isinstance(xtok, list) else [xtok]
        ssq, ssqt = new_stat()
        op("act", lambda e: e.activation(out=junk[0:tsz, :], in_=x_ap, func=AF.Square, scale=1.0 / 32.0, accum_out=ssq[0:tsz, :]),
           reads=xtoks, writes=[t_pT[0], t_pT[1], ssqt])
        op("act", lambda e: e.activation(out=ssq[0:tsz, :], in_=ssq[0:tsz, :], func=AF.Ln, bias=EPS, scale=1.0),
           reads=[ssqt], writes=[ssqt])
        op("act", lambda e: e.activation(out=ssq[0:tsz, :], in_=ssq[0:tsz, :], func=AF.Exp, scale=-0.5),
           reads=[ssqt], writes=[ssqt])
        return ssq, ssqt

    def norm_to_hT(x_ap, tsz, xtok, gi, col0):
        xtoks = xtok if isinstance(xtok, list) else [xtok]
        rstd, rt = rms_stats(x_ap, tsz, xtok)
        j = nxt("xn", 2)
        op("dve", lambda e: e.scalar_tensor_tensor(out=xn[0:tsz, j, :], in0=x_ap, scalar=rstd[0:tsz, :], in1=gb[0:tsz, gi, :],
                                                     op0=ALU.mult, op1=ALU.mult),
           reads=xtoks + [rt, t_gb[gi]], writes=[t_xn[j]])
        for k in range(KD):
            op("pe", lambda e, k=k: e.transpose(out=psT[:, k * 128:k * 128 + tsz], in_=xn[0:tsz, j, k * 128:(k + 1) * 128],
                                                identity=ident[0:tsz, 0:tsz]),
               reads=[t_xn[j], t_const], writes=[t_ps[7]])
        src = psT.rearrange("p (k c) -> p k c", k=KD)[:, :, 0:tsz]
        copy_op(ev_eng(), hT[:, :, col0:col0 + tsz], src, [t_ps[7]], [t_hT])

    gemm_bank = [0]

    def nbank(pool=(0, 1, 2, 3, 4, 5, 6)):
        b = pool[gemm_bank[0] % len(pool)]
        gemm_bank[0] += 1
        return b

    def gemm_fm(bank, wv, c0, M, K, rhs_fn, ntok, rtoks, kpart=128):
        for k in range(K):
            op("pe", lambda e, k=k: e.matmul(ps[bank][0:M, 0:ntok], lhsT=wv[0:kpart, k, c0:c0 + M], rhs=rhs_fn(k),
                                             start=(k == 0), stop=(k == K - 1)),
               reads=rtoks, writes=[t_ps[bank]])

    def gemm_tm(bank, lhs_fn, tsz, wv, c0, ncol, K, rtoks, ktoks=None):
        for k in range(K):
            op("pe", lambda e, k=k: e.matmul(ps[bank][0:tsz, 0:ncol], lhsT=lhs_fn(k), rhs=wv[:, k, c0:c0 + ncol],
                                             start=(k == 0), stop=(k == K - 1)),
               reads=rtoks + ([ktoks[k]] if ktoks is not None else []), writes=[t_ps[bank]])

    def store_rows(bank, tsz, ncol, dst_ap, extra=None):
        i = nxt("stg", 3)
        copy_op(ev_eng(), stg[0:tsz, i, 0:ncol], ps[bank][0:tsz, 0:ncol], [t_ps[bank]], [t_stg[i]])
        if extra is not None:
            extra(stg[0:tsz, i, 0:ncol], t_stg[i])
        op("act", lambda e: e.dma_start(out=dst_ap, in_=stg[0:tsz, i, 0:ncol]), reads=[t_stg[i]], chan=c_stg[i])

    def mem_kv(b):
        gi = load_gain(1)
        for i in range(2):
            op("pool", lambda e, i=i: e.dma_start(out=xt[:, i, :], in_=memp[b, i * 128:(i + 1) * 128, :]),
               writes=[t_xt[i]], chan=c_xt[i])
            norm_to_hT(xt[:, i, :], 128, t_xt[i], gi, i * 128)
        for half in range(2):
            s = wload(("mem", half))
            wv = wview(s, 0, KD, 512)
            if half == 0:
                for c in range(4):
                    bk = nbank()
                    gemm_fm(bk, wv, c * 128, 128, KD, lambda k: hT[:, k, 0:256], 256, [t_ring[s], t_hT])
                    copy_op(ev_eng(), mkT[:, c, :], ps[bk][:, 0:256], [t_ps[bk]], [t_mkT])
            for i in range(2):
                bk = nbank()
                gemm_tm(bk, lambda k, i=i: hT[:, k, i * 128:(i + 1) * 128], 128, wv, 0, 512, KD, [t_ring[s], t_hT])
                dst = (pm_k if half == 0 else pm_v)[b, i * 128:(i + 1) * 128, :]
                if half == 1:
                    def extra(sap, stok, i=i):
                        op("pool", lambda e: e.tensor_copy(out=mv[:, i, :], in_=sap), reads=[stok], writes=[t_mv])
                    store_rows(bk, 128, 512, dst, extra)
                else:
                    store_rows(bk, 128, 512, dst)

    def act_recip(dst, dtok, src, stoks):
        op("act", lambda e: e.activation(out=dst, in_=src, func=AF.Ln), reads=stoks, writes=[dtok])
        op("act", lambda e: e.activation(out=dst, in_=dst, func=AF.Exp, scale=-1.0), reads=[dtok], writes=[dtok])

    def recip_mul(bank_o, bank_z, M, ncols, out_ap, out_tok, c0=0):
        rz, rzt = new_wk()
        act_recip(rz[0:M, 0:ncols], rzt, ps[bank_z][0:M, c0:c0 + ncols], [t_ps[bank_z]])
        op("dve", lambda e: e.tensor_tensor(out=out_ap, in0=ps[bank_o][0:M, c0:c0 + ncols], in1=rz[0:M, 0:ncols], op=ALU.mult),
           reads=[t_ps[bank_o], rzt], writes=[out_tok])

    def diff_finish(o0, o0t, o1, o1t, ncols, out_ap, out_tok):
        d, dt_ = new_wk()
        op("dve", lambda e: e.scalar_tensor_tensor(out=d[:, 0:ncols], in0=o1[:, 0:ncols], scalar=nlam[:, 0:1], in1=o0[:, 0:ncols],
                                                     op0=ALU.mult, op1=ALU.add),
           reads=[o0t, o1t, t_const], writes=[dt_])
        sq, sqt = new_wk()
        op("act", lambda e: e.activation(out=sq[:, 0:ncols], in_=d[:, 0:ncols], func=AF.Square), reads=[dt_], writes=[sqt])
        bk = 7
        op("pe", lambda e: e.matmul(ps[bk][:, 0:ncols], lhsT=onesf[:, :], rhs=sq[:, 0:ncols], start=True, stop=True),
           reads=[sqt, t_const], writes=[t_ps[bk]])
        op("act", lambda e: e.activation(out=sq[:, 0:ncols], in_=ps[bk][:, 0:ncols], func=AF.Ln, bias=EPS, scale=1.0 / 128.0),
           reads=[t_ps[bk]], writes=[sqt])
        op("act", lambda e: e.activation(out=sq[:, 0:ncols], in_=sq[:, 0:ncols], func=AF.Exp, scale=-0.5), reads=[sqt], writes=[sqt])
        op("dve", lambda e: e.scalar_tensor_tensor(out=out_ap, in0=d[:, 0:ncols], scalar=gsubs[:, 0:1], in1=sq[:, 0:ncols],
                                                     op0=ALU.mult, op1=ALU.mult),
           reads=[dt_, sqt, t_const], writes=[out_tok])


    qaT = lambda c: U[:, c, :]
    qbT = lambda h: U[:, 4 + h, :]
    qmT = lambda h: U[:, 8 + h, :]
    oaT = lambda h: U[0:64, 12 + h, :]

    def run_pipeline(items, depth=2):
        if not items:
            return
        pending = []
        for j in range(min(depth, len(items))):
            items[j][0]()
        for i in range(len(items)):
            if i + depth < len(items):
                items[i + depth][0]()
            fin = items[i][1]()
            while pending and pending[0][0] <= i:
                pending.pop(0)[1]()
            if fin is not None:
                pending.append((i + 2, fin))
        for _, fin in pending:
            fin()

    grp = {"copies": [], "done": set()}

    def emit_copy(i):
        if 0 <= i < len(grp["copies"]) and i not in grp["done"]:
            grp["done"].add(i)
            grp["copies"][i]()

    def items_B_prompt(t):
        items = []
        for h in range(4):
            maps = []
            for m in range(2):
                bo, bz = (0, 1) if (2 * h + m) % 2 == 0 else (2, 3)
                tiles = [(4 * t + j, 128 * j, True) for j in range(4)] + [(kt, 0, False) for kt in range(4 * t)]
                gi_ = len(grp["copies"])

                def cp(h=h, m=m):
                    pb = m * 64
                    op("dve", lambda e: e.tensor_copy(out=qz[pb:pb + 64, m, :], in_=U[pb:pb + 64, 4 + h, :]),
                       reads=[t_U[4 + h]], writes=[t_qz[m]])
                grp["copies"].append(cp)
                for n_, (kt, q0, diag) in enumerate(tiles):
                    stt = {}
                    first = (n_ == 0)
                    last = (n_ == len(tiles) - 1)

                    def qk(kt=kt, q0=q0, stt=stt, h=h, m=m, first=first, gi_=gi_):
                        bs = nbank((4, 5, 6))
                        stt["bs"] = bs
                        pb = m * 64
                        if first:
                            emit_copy(gi_)
                        op("pe", lambda e: e.matmul(
                            ps[bs][:, q0:TS], lhsT=kbT[:, h, kt * 128:(kt + 1) * 128], rhs=qz[:, m, q0:TS],
                            start=True, stop=True), reads=[t_kbT[kt // 4], t_qz[m]], writes=[t_ps[bs]])

                    def rest(kt=kt, q0=q0, diag=diag, stt=stt, h=h, m=m, first=first, last=last, bo=bo, bz=bz, maps=maps, gi_=gi_):
                        bs = stt["bs"]
                        if first:
                            emit_copy(gi_ + 1)
                        pi = nxt("pT", 4)
                        o_ = kt - 4 * t + 12
                        if h == 0:
                            segs_ = [(a_, b_, biasB0[:, i_, o_:o_ + 1]) for i_, (a_, b_) in
                                     enumerate(((q0, 256), (max(q0, 256), TS))) if b_ > a_]
                        else:
                            segs_ = [(q0, TS, biasB[:, h, o_:o_ + 1])]
                        for (a_, b_, bias_) in segs_:
                            op("act", lambda e, a_=a_, b_=b_, bias_=bias_: e.activation(
                                out=pT[:, pi, a_:b_], in_=ps[bs][:, a_:b_], func=AF.Exp, bias=bias_, scale=0.125),
                                reads=[t_ps[bs], t_const], writes=[t_pT[pi]])
                        if diag:
                            op("dve", lambda e: e.tensor_tensor(
                                out=pT[:, pi, q0:q0 + 128], in0=pT[:, pi, q0:q0 + 128], in1=mcorr[:, h, :], op=ALU.mult),
                                reads=[t_pT[pi], t_const], writes=[t_pT[pi]])
                        op("pe", lambda e: e.matmul(
                            ps[bo][:, q0:TS], lhsT=vb[:, kt, h * 128:(h + 1) * 128], rhs=pT[:, pi, q0:TS],
                            start=first, stop=last, skip_group_check=True), reads=[t_vb[kt // 4], t_pT[pi]], writes=[t_ps[bo]])
                        op("pe", lambda e: e.matmul(
                            ps[bz][:, q0:TS], lhsT=ones[:, :], rhs=pT[:, pi, q0:TS],
                            start=first, stop=last, skip_group_check=True), reads=[t_const, t_pT[pi]], writes=[t_ps[bz]])
                        if last:
                            def fin():
                                om_, omt = new_wk()
                                recip_mul(bo, bz, 128, TS, om_[:, :], omt)
                                maps.append((om_, omt))
                                if m == 1:
                                    diff_finish(maps[0][0], maps[0][1], maps[1][0], maps[1][1], TS, obT[:, h, :], t_obT[h])
                            return fin
                        return None
                    items.append((qk, rest))
        return items

    def items_A_prompt(t):
        items = []
        for h in range(8):
            c, pb = h // 2, (h % 2) * 64
            bo, bz = (0, 1) if h % 2 == 0 else (2, 3)
            js = [j for j in (4, 5, 6, 7, 3, 2, 1, 0) if 4 * t - 4 + j >= 0]
            gi_ = len(grp["copies"])

            def cp(h=h, c=c, pb=pb):
                par = h % 2
                op("dve", lambda e: e.tensor_copy(out=qz[pb:pb + 64, par, :], in_=U[pb:pb + 64, c, :]),
                   reads=[t_U[c]], writes=[t_qz[par]])
            grp["copies"].append(cp)
            for n_, j in enumerate(js):
                kt = 4 * t - 4 + j
                first = (n_ == 0)
                last = (n_ == len(js) - 1)
                if j <= 3:
                    q0, q1 = 0, 128 * (j + 1)
                else:
                    q0, q1 = 128 * (j - 4), TS
                stt = {}

                def qk(kt=kt, q0=q0, q1=q1, stt=stt, c=c, pb=pb, h=h, first=first, gi_=gi_):
                    rc = (kt % 8) * 128
                    bs = nbank((4, 5, 6))
                    stt["bs"] = bs
                    par = h % 2
                    if first:
                        emit_copy(gi_)
                    op("pe", lambda e: e.matmul(
                        ps[bs][:, q0:q1], lhsT=kaT[:, c, rc:rc + 128], rhs=qz[:, par, q0:q1],
                        start=True, stop=True), reads=[t_kaT[(kt // 4) % 2], t_qz[par]], writes=[t_ps[bs]])

                def rest(kt=kt, j=j, q0=q0, q1=q1, stt=stt, h=h, first=first, last=last, bo=bo, bz=bz, gi_=gi_):
                    bs = stt["bs"]
                    if first:
                        emit_copy(gi_ + 1)
                    pi = nxt("pT", 4)
                    op("act", lambda e: e.activation(
                        out=pT[:, pi, q0:q1], in_=ps[bs][:, q0:q1], func=AF.Exp, scale=0.125),
                        reads=[t_ps[bs]], writes=[t_pT[pi]])
                    if j >= 3:
                        x0 = q0 - 128 * (j - 4)
                        op("dve", lambda e: e.tensor_tensor(
                            out=pT[:, pi, q0:q1], in0=pT[:, pi, q0:q1], in1=strip[:, h, x0:x0 + (q1 - q0)], op=ALU.mult),
                            reads=[t_pT[pi], t_const], writes=[t_pT[pi]])
                    if j <= 3:
                        r0, cc = 0, 64 * (2 * j + 1)
                    else:
                        r0, cc = 64, 64 * (2 * j - 8)
                    op("dve" if j >= 3 else "pool", lambda e: e.memset(pT[r0:r0 + 64, pi, cc:cc + 64], 0.0),
                       reads=[t_pT[pi]], writes=[t_pT[pi]])
                    op("pe", lambda e: e.matmul(
                        ps[bo][:, q0:q1], lhsT=va[:, kt % 8, h * 64:h * 64 + 128], rhs=pT[:, pi, q0:q1],
                        start=first, stop=last, skip_group_check=True), reads=[t_va[(kt // 4) % 2], t_pT[pi]], writes=[t_ps[bo]])
                    op("pe", lambda e: e.matmul(
                        ps[bz][:, q0:q1], lhsT=ones[:, :], rhs=pT[:, pi, q0:q1],
                        start=first, stop=last, skip_group_check=True), reads=[t_const, t_pT[pi]], writes=[t_ps[bz]])
                    if last:
                        return lambda: recip_mul(bo, bz, 64, TS, U[0:64, 12 + h, :], t_U[12 + h])
                    return None
                items.append((qk, rest))
        return items

    def items_M(ntok, qc):
        items = []
        for h in range(4):
            bo, bz = (0, 1) if h % 2 == 0 else (2, 3)
            for mt in range(2):
                stt = {}

                def qk(mt=mt, stt=stt, h=h):
                    bs = nbank((4, 5, 6))
                    stt["bs"] = bs
                    op("pe", lambda e: e.matmul(
                        ps[bs][:, 0:ntok], lhsT=mkT[:, h, mt * 128:(mt + 1) * 128], rhs=U[:, 8 + h, qc:qc + ntok],
                        start=True, stop=True), reads=[t_mkT, t_U[8 + h]], writes=[t_ps[bs]])

                def rest(mt=mt, stt=stt, h=h, bo=bo, bz=bz):
                    bs = stt["bs"]
                    pi = nxt("pT", 4)
                    op("act", lambda e: e.activation(
                        out=pT[:, pi, 0:ntok], in_=ps[bs][:, 0:ntok], func=AF.Exp, scale=float(128 ** -0.5)),
                        reads=[t_ps[bs]], writes=[t_pT[pi]])
                    op("pe", lambda e: e.matmul(
                        ps[bo][:, 0:ntok], lhsT=mv[:, mt, h * 128:(h + 1) * 128], rhs=pT[:, pi, 0:ntok],
                        start=(mt == 0), stop=(mt == 1)), reads=[t_mv, t_pT[pi]], writes=[t_ps[bo]])
                    op("pe", lambda e: e.matmul(
                        ps[bz][:, 0:ntok], lhsT=ones[:, :], rhs=pT[:, pi, 0:ntok],
                        start=(mt == 0), stop=(mt == 1)), reads=[t_const, t_pT[pi]], writes=[t_ps[bz]])
                    if mt == 1:
                        return lambda: recip_mul(bo, bz, 128, ntok, omT[:, h, qc:qc + ntok], t_omT[h])
                    return None
                items.append((qk, rest))
        return items

    def load_ktile(src_rows, pre=None):
        ki = nxt("kt", 2)
        if pre is not None:
            op("sp", lambda e: e.dma_start(out=ktile[:, ki, :], in_=src_rows), writes=[t_ktile[ki]], chan=c_ktile2[ki],
               extra_deps=[pre])
        else:
            op("pool", lambda e: e.dma_start(out=ktile[:, ki, :], in_=src_rows), writes=[t_ktile[ki]], chan=c_ktile[ki])
        for c in range(4):
            op("pe", lambda e, c=c: e.transpose(out=psT[:, c * 128:(c + 1) * 128], in_=ktile[:, ki, c * 128:(c + 1) * 128],
                                                identity=ident[:, :]),
               reads=[t_ktile[ki], t_const], writes=[t_ps[7]])
        copy_op(ev_eng(), kTt[:, ki, :], psT[:, 0:512], [t_ps[7]], [t_kTt[ki]])
        return ki

    def load_vtile(src_rows, pre=None):
        vi = nxt("vt", 2)
        if pre is not None:
            op("pool", lambda e: e.dma_start(out=vtile[:, vi, :], in_=src_rows), writes=[t_vtile[vi]], chan=c_vtile[vi],
               extra_deps=[pre])
        else:
            op("pool", lambda e: e.dma_start(out=vtile[:, vi, :], in_=src_rows), writes=[t_vtile[vi]], chan=c_vtile[vi])
        return vi

    def attn_B_sample(b):
        qc = b * 64
        SP = (2, 3, 4, 5, 6)
        qzs_v = qzs[:].rearrange("p (h m) q -> p h m q", m=2)
        for m in range(2):
            op("pool", lambda e, m=m: e.tensor_copy(out=qzs_v[m * 64:(m + 1) * 64, :, m, :], in_=U[m * 64:(m + 1) * 64, 4:8, qc:qc + 64]),
               reads=t_U[4:8], writes=[t_qzs])
        items = []
        for kt in range(33):
            last = (kt == 32)
            nk = 64 if last else 128
            stt = {}

            def qk(kt=kt, last=last, nk=nk, stt=stt):
                if not last:
                    ki = load_ktile(cb16[0, b, kt * 128:(kt + 1) * 128, :], pre=cc_ops[(0, b, kt // 4)])
                    vi = load_vtile(cb16[1, b, kt * 128:(kt + 1) * 128, :], pre=cc_ops[(1, b, kt // 4)])
                    kT = lambda h: kTt[:, ki, h * 128:(h + 1) * 128]
                    vv = lambda h: vtile[:, vi, h * 128:(h + 1) * 128]
                    krd = [t_kTt[ki]]
                    vrd = [t_vtile[vi]]
                else:
                    kT = lambda h: kbT[:, h, qc:qc + 64]
                    vv = lambda h: vb[0:64, b, h * 128:(h + 1) * 128]
                    krd = [t_kbT[0]]
                    vrd = [t_vb[0]]
                bs = nbank(SP)
                for g in range(8):
                    op("pe", lambda e, g=g: e.matmul(
                        ps[bs][0:nk, g * 64:(g + 1) * 64], lhsT=kT(g // 2), rhs=qzs[:, g, :],
                        start=True, stop=True, skip_group_check=True), reads=krd + [t_qzs], writes=[t_ps[bs]])
                stt.update(bs=bs, vv=vv, vrd=vrd)

            def rest(kt=kt, last=last, nk=nk, stt=stt):
                bs, vv, vrd = stt["bs"], stt["vv"], stt["vrd"]
                pi = nxt("pT", 4)
                for h in range(4):
                    op("act", lambda e, h=h: e.activation(
                        out=pT[0:nk, pi, h * 128:(h + 1) * 128], in_=ps[bs][0:nk, h * 128:(h + 1) * 128], func=AF.Exp,
                        bias=biasS[0:nk, h, kt:kt + 1], scale=0.125), reads=[t_ps[bs], t_const], writes=[t_pT[pi]])
                if last:
                    for g in range(8):
                        op("dve", lambda e, g=g: e.tensor_tensor(
                            out=pT[0:64, pi, g * 64:(g + 1) * 64], in0=pT[0:64, pi, g * 64:(g + 1) * 64],
                            in1=mcorr[0:64, g // 2, 0:64], op=ALU.mult), reads=[t_pT[pi], t_const], writes=[t_pT[pi]])
                for g in range(8):
                    op("pe", lambda e, g=g: e.matmul(
                        ps[0][:, g * 64:(g + 1) * 64], lhsT=vv(g // 2), rhs=pT[0:nk, pi, g * 64:(g + 1) * 64],
                        start=(kt == 0 and g == 0), stop=last, skip_group_check=True),
                        reads=vrd + [t_pT[pi]], writes=[t_ps[0]])
                op("pe", lambda e: e.matmul(ps[1][:, 0:512], lhsT=ones[0:nk, :], rhs=pT[0:nk, pi, 0:512],
                                            start=(kt == 0), stop=last, skip_group_check=True),
                   reads=[t_const, t_pT[pi]], writes=[t_ps[1]])
                return None
            items.append((qk, rest))
        run_pipeline(items, depth=1)
        rz, rzt = new_wk()
        act_recip(rz[:, :], rzt, ps[1][:, 0:512], [t_ps[1]])
        on, ont = new_wk()
        op("dve", lambda e: e.tensor_tensor(out=on[:, :], in0=ps[0][:, 0:512], in1=rz[:, :], op=ALU.mult),
           reads=[t_ps[0], rzt], writes=[ont])
        for h in range(4):
            o0 = on[:, (2 * h) * 64:(2 * h + 1) * 64]
            o1 = on[:, (2 * h + 1) * 64:(2 * h + 2) * 64]
            diff_finish(o0, ont, o1, ont, 64, obT[:, h, qc:qc + 64], t_obT[h])

    def attn_A_sample(b):
        qc = b * 64
        for j in range(5):
            last = (j == 4)
            nk = 64 if last else 128
            if not last:
                ki = load_ktile(cak[b, j * 128:(j + 1) * 128, :])
                ckpt(40)
                vi = load_vtile(cav[b, j * 128:(j + 1) * 128, :])
                ckpt(41)
                kT = lambda h: kTt[(h % 2) * 64:(h % 2) * 64 + 64, ki, (h // 2) * 128:(h // 2 + 1) * 128]
                vv = lambda h: vtile[:, vi, h * 64:(h + 1) * 64]
                krd = [t_kTt[ki]]
                vrd = [t_vtile[vi]]
            else:
                kT = lambda h: kaT[(h % 2) * 64:(h % 2) * 64 + 64, h // 2, qc:qc + 64]
                vv = lambda h: va[0:64, b, h * 64:(h + 1) * 64]
                krd = [t_kaT[0]]
                vrd = [t_va[0]]
            bsp = (nbank((4, 5, 6)), nbank((4, 5, 6)))
            colA = lambda h: (h % 2) * 256 + (h // 2) * 64
            for h in range(8):
                pb = (h % 2) * 64
                op("pe", lambda e, h=h, pb=pb: e.matmul(
                    ps[bsp[h % 2]][0:nk, (h // 2) * 64:(h // 2 + 1) * 64], lhsT=kT(h), rhs=U[pb:pb + 64, h // 2, qc:qc + 64],
                    start=True, stop=True, skip_group_check=True), reads=krd + [t_U[h // 2]], writes=[t_ps[bsp[h % 2]]])
            pi = nxt("pT", 4)
            for par in range(2):
                op("act", lambda e, par=par: e.activation(out=pT[0:nk, pi, par * 256:(par + 1) * 256],
                                                          in_=ps[bsp[par]][0:nk, 0:256], func=AF.Exp, scale=0.125),
                   reads=[t_ps[bsp[par]]], writes=[t_pT[pi]])
            if j >= 3:
                x0 = 128 if j == 3 else 0
                for h in range(8):
                    op("dve", lambda e, h=h: e.tensor_tensor(
                        out=pT[0:nk, pi, colA(h):colA(h) + 64], in0=pT[0:nk, pi, colA(h):colA(h) + 64],
                        in1=strip[0:nk, h, x0:x0 + 64], op=ALU.mult), reads=[t_pT[pi], t_const], writes=[t_pT[pi]])
            for h in range(8):
                op("pe", lambda e, h=h: e.matmul(
                    ps[0][0:64, colA(h):colA(h) + 64], lhsT=vv(h), rhs=pT[0:nk, pi, colA(h):colA(h) + 64],
                    start=(j == 0 and h == 0), stop=last, skip_group_check=True), reads=vrd + [t_pT[pi]], writes=[t_ps[0]])
            op("pe", lambda e: e.matmul(ps[1][0:64, 0:512], lhsT=ones[0:nk, 0:64], rhs=pT[0:nk, pi, 0:512],
                                        start=(j == 0), stop=last, skip_group_check=True),
               reads=[t_const, t_pT[pi]], writes=[t_ps[1]])
        rz, rzt = new_wk()
        act_recip(rz[0:64, :], rzt, ps[1][0:64, 0:512], [t_ps[1]])
        for h in range(8):
            ca = (h % 2) * 256 + (h // 2) * 64
            op("dve", lambda e, h=h, ca=ca: e.tensor_tensor(out=U[0:64, 12 + h, qc:qc + 64], in0=ps[0][0:64, ca:ca + 64],
                                                             in1=rz[0:64, ca:ca + 64], op=ALU.mult),
               reads=[t_ps[0], rzt], writes=[t_U[12 + h]])

    def load_mem_sample(b):
        for mt in range(2):
            ki = load_ktile(cmk[b, mt * 128:(mt + 1) * 128, :])
            for h in range(4):
                copy_op("pool", mkT[:, h, mt * 128:(mt + 1) * 128], kTt[:, ki, h * 128:(h + 1) * 128], [t_kTt[ki]], [t_mkT])
            op("pool", lambda e, mt=mt: e.dma_start(out=mv[:, mt, :], in_=cmv[b, mt * 128:(mt + 1) * 128, :]),
               writes=[t_mv], chan=c_miscp)

    def step(kind, b, t, preloaded=False, next_x=None, gi_pre=None, has_next=False, norm_done=False):
        prm = (kind == "p")
        if prm:
            ntok = TS
            tiles = [(i * 128, 128) for i in range(4)]
            xsrc = lambda i: xp[b, t * TS + i * 128:t * TS + (i + 1) * 128, :]
            segs = [(0, TS, 0)]
        else:
            ntok = 128
            tiles = [(0, 64), (64, 64)]
            xsrc = lambda i: xs[i, :, :]
            segs = [(0, 64, 0), (64, 64, 1)]
        gi = gi_pre if gi_pre is not None else load_gain(0)
        gi_ffn = load_gain(2)
        for i, (c0, tsz) in enumerate(tiles):
            if norm_done:
                break
            if not preloaded:
                op("pool", lambda e, i=i, tsz=tsz: e.dma_start(out=xt[0:tsz, i, :], in_=xsrc(i)), writes=[t_xt[i]], chan=c_xt[i])
            norm_to_hT(xt[0:tsz, i, :], tsz, t_xt[i], gi, c0)

        gi_fin = load_gain(3)
        if prm:
            issue_cache_conv(4)
        else:
            issue_cache_conv(len(cc_list))

        def fm_group(gidx, dst_fn, dst_tok_fn):
            s = wload(("in", gidx))
            wv = wview(s, 0, KD, 512)
            for c in range(4):
                bk = nbank()
                gemm_fm(bk, wv, c * 128, 128, KD, lambda k: hT[:, k, 0:ntok], ntok, [t_ring[s], t_hT])
                copy_op(ev_eng(), dst_fn(c), ps[bk][:, 0:ntok], [t_ps[bk]], [dst_tok_fn(c)])
            return s, wv

        def tm_rows(s, wv, dst_fn, extra_fn=None, direct_fn=None):
            for i, (c0, tsz) in enumerate(tiles):
                bk = nbank()
                gemm_tm(bk, lambda k: hT[:, k, c0:c0 + tsz], tsz, wv, 0, 512, KD, [t_ring[s], t_hT])
                if direct_fn is not None:
                    direct_fn(i, bk, tsz)
                else:
                    store_rows(bk, tsz, 512, dst_fn(i), extra_fn(i, tsz) if extra_fn else None)

        def pool_copy_to(dst_ap, dst_tok):
            def extra(sap, stok):
                op("pool", lambda e: e.tensor_copy(out=dst_ap, in_=sap), reads=[stok], writes=[dst_tok])
            return extra

        if prm:
            hf = t % 2
            s, wv = fm_group(1, lambda c: kaT[:, c, hf * TS:(hf + 1) * TS], lambda c: t_kaT[hf])
            if t == NSTEP - 1:
                tm_rows(s, wv, lambda i: pa_k[b, i * 128:(i + 1) * 128, :])
            s, wv = fm_group(4, lambda c: kbT[:, c, t * TS:(t + 1) * TS], lambda c: t_kbT[t])
            tm_rows(s, wv, lambda i: pb_k[b, t * TS + i * 128:t * TS + (i + 1) * 128, :])
            s = wload(("in", 2))
            wv = wview(s, 0, KD, 512)
            if t == NSTEP - 1:
                tm_rows(s, wv, lambda i: pa_v[b, i * 128:(i + 1) * 128, :],
                        extra_fn=lambda i, tsz: pool_copy_to(va[:, (4 * t + i) % 8, 0:512], t_va[hf]))
            else:
                tm_rows(s, wv, None, direct_fn=lambda i, bk, tsz: copy_op(
                    ev_eng(), va[:, (4 * t + i) % 8, 0:512], ps[bk][:, 0:512], [t_ps[bk]], [t_va[hf]]))
            s = wload(("in", 5))
            wv = wview(s, 0, KD, 512)
            tm_rows(s, wv, lambda i: pb_v[b, t * TS + i * 128:t * TS + (i + 1) * 128, :],
                    extra_fn=lambda i, tsz: pool_copy_to(vb[:, 4 * t + i, :], t_vb[t]))
        else:
            s, wv = fm_group(1, lambda c: kaT[:, c, 0:128], lambda c: t_kaT[0])
            tm_rows(s, wv, lambda i: sa_k[i, 448:512, :])
            s, wv = fm_group(4, lambda c: kbT[:, c, 0:128], lambda c: t_kbT[0])
            tm_rows(s, wv, lambda i: sb_k[i, :, :])
            s = wload(("in", 2))
            wv = wview(s, 0, KD, 512)
            tm_rows(s, wv, lambda i: sa_v[i, 448:512, :], extra_fn=lambda i, tsz: pool_copy_to(va[0:64, i, 0:512], t_va[0]))
            s = wload(("in", 5))
            wv = wview(s, 0, KD, 512)
            tm_rows(s, wv, lambda i: sb_v[i, :, :], extra_fn=lambda i, tsz: pool_copy_to(vb[0:64, i, :], t_vb[0]))
            for i in range(2):
                op("sp", lambda e, i=i: e.dma_start(out=sa_k[i, 0:448, :], in_=cak[i, 64:512, :]), chan=c_d2d)
                op("sp", lambda e, i=i: e.dma_start(out=sa_v[i, 0:448, :], in_=cav[i, 64:512, :]), chan=c_d2d)
        ckpt(3)
        fm_group(0, lambda c: U[:, c, 0:ntok], lambda c: t_U[c])
        fm_group(3, lambda c: U[:, 4 + c, 0:ntok], lambda c: t_U[4 + c])
        fm_group(6, lambda c: U[:, 8 + c, 0:ntok], lambda c: t_U[8 + c])
        ckpt(4)
        if prm:
            grp["copies"] = []
            grp["done"] = set()
            run_pipeline(items_A_prompt(t) + items_B_prompt(t) + items_M(TS, 0))
            ckpt(7)
        else:
            for bb in range(2):
                attn_A_sample(bb)
                ckpt(30)
                attn_B_sample(bb)
                ckpt(31)
                load_mem_sample(bb)
                run_pipeline(items_M(64, bb * 64))
                ckpt(32)
        for f in range(8):
            s = wload(("mg", f))
            wva = wview(s, 0, 8, 128)
            wvb = wview(s, 1024, 4, 128)
            wvm = wview(s, 1536, 4, 128)
            wvg = wview(s, 2048, KD, 384)
            ba = nbank()
            gemm_fm(ba, wva, 0, 128, 8, lambda k: U[0:64, 12 + k, 0:ntok], ntok, [t_ring[s]] + t_U[12:20], kpart=64)
            bb_ = nbank()
            gemm_fm(bb_, wvb, 0, 128, 4, lambda k: obT[:, k, 0:ntok], ntok, [t_ring[s]] + t_obT)
            bm = nbank()
            gemm_fm(bm, wvm, 0, 128, 4, lambda k: omT[:, k, 0:ntok], ntok, [t_ring[s]] + t_omT)
            prods = []
            for bb, bo_ in enumerate((ba, bb_, bm)):
                bg_ = nbank()
                gemm_fm(bg_, wvg, bb * 128, 128, KD, lambda k: hT[:, k, 0:ntok], ntok, [t_ring[s], t_hT])
                gt, gtt = new_wk()
                op("act", lambda e, bb=bb, bg_=bg_, gt=gt: e.activation(
                    out=gt[:, 0:ntok], in_=ps[bg_][:, 0:ntok], func=AF.Sigmoid, bias=bg[:, bb * 8 + f:bb * 8 + f + 1]),
                    reads=[t_ps[bg_], t_const], writes=[gtt])
                op("dve", lambda e, bo_=bo_, gt=gt: e.tensor_tensor(
                    out=gt[:, 0:ntok], in0=ps[bo_][:, 0:ntok], in1=gt[:, 0:ntok], op=ALU.mult),
                    reads=[t_ps[bo_], gtt], writes=[gtt])
                prods.append((gt, gtt))
            (p0, p0t), (p1, p1t), (p2, p2t) = prods
            op("pool", lambda e: e.tensor_tensor(out=p0[:, 0:ntok], in0=p0[:, 0:ntok], in1=p1[:, 0:ntok], op=ALU.add),
               reads=[p0t, p1t], writes=[p0t])
            op("pool", lambda e: e.tensor_tensor(out=U[:, f, 0:ntok], in0=p0[:, 0:ntok], in1=p2[:, 0:ntok], op=ALU.add),
               reads=[p0t, p2t], writes=[t_U[f]])
        ckpt(8)
        so = [wload(("out", cg)) for cg in range(2)]
        for i, (c0, tsz) in enumerate(tiles):
            for cg in range(2):
                wv = wview(so[cg], 0, KD, 512)
                bk = nbank()
                gemm_tm(bk, lambda k: U[:, k, c0:c0 + tsz], tsz, wv, 0, 512, KD, [t_ring[so[cg]]], ktoks=t_U)
                op("dve", lambda e, i=i, tsz=tsz, bk=bk, cg=cg: e.tensor_tensor(
                    out=xt[0:tsz, i, cg * 512:(cg + 1) * 512], in0=ps[bk][0:tsz, 0:512],
                    in1=xt[0:tsz, i, cg * 512:(cg + 1) * 512], op=ALU.add), reads=[t_ps[bk], t_xt[i]], writes=[t_xt[i]])
        ckpt(9)
        for i, (c0, tsz) in enumerate(tiles):
            norm_to_hT(xt[0:tsz, i, :], tsz, t_xt[i], gi_ffn, c0)
        gi_next = load_gain(0) if has_next else None
        for cp in range(NFF // 2):
            s = wload(("up", cp))
            wa = wview(s, 0, KD, 256)
            wg = wview(s, 2048, KD, 256)
            for cc in range(2):
                c = cp * 2 + cc
                bka = nbank()
                gemm_fm(bka, wa, cc * 128, 128, KD, lambda k: hT[:, k, 0:ntok], ntok, [t_ring[s], t_hT])
                bkg = nbank()
                gemm_fm(bkg, wg, cc * 128, 128, KD, lambda k: hT[:, k, 0:ntok], ntok, [t_ring[s], t_hT])
                ai = nxt("abuf", 3)
                cv, cvt = new_wk()
                for (q0, n, hb) in segs:
                    a0 = q0 + 2 * hb
                    op("pool", lambda e, a0=a0, hb=hb: e.tensor_copy(out=abuf[:, ai, a0:a0 + 2], in_=ahist[:, hb, c, :]),
                       reads=[t_ahist], writes=[t_abuf[ai]])
                    op("act", lambda e, a0=a0, q0=q0, n=n: e.activation(out=abuf[:, ai, a0 + 2:a0 + 2 + n], in_=ps[bka][:, q0:q0 + n],
                                                                        func=AF.Copy),
                       reads=[t_ps[bka]], writes=[t_abuf[ai]])
                    op("pool", lambda e, a0=a0, n=n, hb=hb: e.tensor_copy(out=ahist[:, hb, c, :], in_=abuf[:, ai, a0 + n:a0 + n + 2]),
                       reads=[t_abuf[ai]], writes=[t_ahist])
                    op("dve", lambda e, a0=a0, q0=q0, n=n: e.tensor_scalar(
                        out=cv[:, q0:q0 + n], in0=abuf[:, ai, a0 + 2:a0 + 2 + n], scalar1=wc[:, c, 2:3], scalar2=bc[:, c:c + 1],
                        op0=ALU.mult, op1=ALU.add), reads=[t_abuf[ai], t_const], writes=[cvt])
                    op("dve", lambda e, a0=a0, q0=q0, n=n: e.scalar_tensor_tensor(
                        out=cv[:, q0:q0 + n], in0=abuf[:, ai, a0 + 1:a0 + 1 + n], scalar=wc[:, c, 1:2], in1=cv[:, q0:q0 + n],
                        op0=ALU.mult, op1=ALU.add), reads=[t_abuf[ai], t_const, cvt], writes=[cvt])
                    op("dve", lambda e, a0=a0, q0=q0, n=n: e.scalar_tensor_tensor(
                        out=cv[:, q0:q0 + n], in0=abuf[:, ai, a0:a0 + n], scalar=wc[:, c, 0:1], in1=cv[:, q0:q0 + n],
                        op0=ALU.mult, op1=ALU.add), reads=[t_abuf[ai], t_const, cvt], writes=[cvt])
                op("act", lambda e: e.activation(out=cv[:, 0:ntok], in_=cv[:, 0:ntok], func=AF.Gelu_apprx_tanh),
                   reads=[cvt], writes=[cvt])
                op("dve", lambda e: e.tensor_tensor(out=U[:, c, 0:ntok], in0=ps[bkg][:, 0:ntok], in1=cv[:, 0:ntok], op=ALU.mult),
                   reads=[t_ps[bkg], cvt], writes=[t_U[c]])
        ckpt(10)
        if next_x is not None:
            for i, (ntsz, nsrc) in enumerate(next_x):
                op("act", lambda e, i=i, ntsz=ntsz, nsrc=nsrc: e.dma_start(out=xal[i][0:ntsz, :], in_=nsrc),
                   writes=xal_tok[i], chan=c_xal[i])
        if (prm and t == NSTEP - 1) or not prm:
            for hb in ([0] if prm else [0, 1]):
                bk = nbank()
                for j in range(2):
                    op("pe", lambda e, j=j: e.transpose(out=ps[bk][0:NFF, j * 128:(j + 1) * 128], in_=ahist[:, hb, :, j],
                                                        identity=identf[:, :]),
                       reads=[t_ahist, t_const], writes=[t_ps[bk]])
                dst = (pconv[b] if prm else sconv[hb]).rearrange("j (c p) -> c j p", p=128)
                i_ = nxt("stg", 3)
                copy_op(ev_eng(), stg[0:NFF, i_, 0:256], ps[bk][0:NFF, 0:256], [t_ps[bk]], [t_stg[i_]])
                op("act", lambda e, i_=i_: e.dma_start(out=dst, in_=stg[0:NFF, i_, 0:256].rearrange("c (j p) -> c j p", j=2)),
                   reads=[t_stg[i_]], chan=c_stg[i_])
        for cg in range(4):
            s = wload(("dn", cg))
            wv = wview(s, 0, NFF, 256)
            for i, (c0, tsz) in enumerate(tiles):
                bk = nbank()
                gemm_tm(bk, lambda k: U[:, k, c0:c0 + tsz], tsz, wv, 0, 256, NFF, [t_ring[s]], ktoks=t_U)
                op("dve", lambda e, i=i, tsz=tsz, bk=bk: e.tensor_tensor(
                    out=xt[0:tsz, i, cg * 256:(cg + 1) * 256], in0=ps[bk][0:tsz, 0:256],
                    in1=xt[0:tsz, i, cg * 256:(cg + 1) * 256], op=ALU.add), reads=[t_ps[bk], t_xt[i]], writes=[t_xt[i]])
        ckpt(11)
        if next_x is not None:
            noff = 0
            for i, (ntsz, nsrc) in enumerate(next_x):
                norm_to_hT(xal[i][0:ntsz, :], ntsz, xal_tok[i], gi_next, noff)
                noff += ntsz
        gi = gi_fin
        for i, (c0, tsz) in enumerate(tiles):
            rstd, rt = rms_stats(xt[0:tsz, i, :], tsz, t_xt[i])
            op("dve", lambda e, i=i, tsz=tsz: e.scalar_tensor_tensor(
                out=xt[0:tsz, i, :], in0=xt[0:tsz, i, :], scalar=rstd[0:tsz, :], in1=gb[0:tsz, gi, :],
                op0=ALU.mult, op1=ALU.mult), reads=[t_xt[i], rt, t_gb[gi]], writes=[t_xt[i]])
            dst = y_p[b, t * TS + c0:t * TS + c0 + tsz, :] if prm else y_s[i, :, :]
            op("act", lambda e, i=i, tsz=tsz, dst=dst: e.dma_start(out=dst, in_=xt[0:tsz, i, :]), reads=[t_xt[i]], chan=c_yst[i])
            if next_x is not None and i < len(next_x):
                ntsz, nsrc = next_x[i]
                op("pool", lambda e, i=i, ntsz=ntsz: e.tensor_copy(out=xt[0:ntsz, i, :], in_=xal[i][0:ntsz, :]),
                   reads=xal_tok[i], writes=[t_xt[i]])
        return gi_next

    ckpt(1)
    gi_carry = None
    for b in range(2):
        if ONLY[0] == "s":
            break
        mem_kv(b)
        if b == 0:
            late_constants()
        ckpt(2)
        op("pool", lambda e: e.memset(ahist[:, 0, :, :], 0.0), writes=[t_ahist])
        for t in range(NSTEP):
            if t < NSTEP - 1:
                nx = [(128, xp[b, (t + 1) * TS + i * 128:(t + 1) * TS + (i + 1) * 128, :]) for i in range(4)]
            elif b == 1:
                nx = [(64, xs[i, :, :]) for i in range(2)]
            else:
                nx = None
            if t == 0:
                gi_carry = None
            gi_carry = step("p", b, t, preloaded=(t > 0), next_x=nx, gi_pre=gi_carry,
                            has_next=(t < NSTEP - 1 or b == 1), norm_done=(t > 0))
            ckpt(12 + t)
        ckpt(20)
    op("sp", lambda e: e.dma_start(out=ahist[:, :, :, :], in_=sconvT[:, :, :, :]), writes=[t_ahist], chan=c_misc)
    step("s", 0, 0, preloaded=(ONLY[0] != "s"), gi_pre=(gi_carry if ONLY[0] != "s" else None), norm_done=(ONLY[0] != "s"))


_CACHE = {}


def kernel(x_prompt, x_sample, mem_prompt, cache_a_k, cache_a_v, cache_b_k, cache_b_v,
           cache_mem_k, cache_mem_v, state_ffn_conv, g_mix, w_in, b_gate, rel_bias,
           lam_q, lam_k, g_sub, g_mem, w_mem_kv, w_oa, w_ob, w_om, w_out, g_ffn,
           w_up, w_conv, b_conv, w_down, g_final):
    f = lambda a: np.ascontiguousarray(np.asarray(a, dtype=np.float32))
    if "nc" not in _CACHE:
        _CACHE["nc"] = build_nc()
    nc, _es = _CACHE["nc"]
    g4 = f(np.stack([np.tile(np.asarray(g).reshape(1, D), (128, 1)) for g in (g_mix[0], g_mem[0], g_ffn[0], g_final)]))
    bgT = f(np.asarray(b_gate[0]).reshape(24, 128).T)
    wcT = f(np.asarray(w_conv[0]).reshape(3, NFF, 128).transpose(2, 1, 0))
    bcT = f(np.asarray(b_conv[0]).reshape(NFF, 128).T)
    gsub = f(np.asarray(g_sub[0]).reshape(128, 1))
    lamq = f(np.tile(np.asarray(lam_q[0]).reshape(1, 128), (128, 1)))
    lamk = f(np.tile(np.asarray(lam_k[0]).reshape(1, 128), (128, 1)))
    p_ = np.arange(128)[:, None]
    x_ = np.arange(640)[None, :]
    idx = np.clip(x_ - p_, -128, 128) + 128
    rb = np.asarray(rel_bias[0])
    relT = f(rb[:, idx].transpose(1, 0, 2))
    rbfar = f(np.tile(rb[:, 256].reshape(1, 8), (128, 1)))
    ident = np.eye(128, dtype=np.float32)
    kmq = f(p_ - np.arange(128)[None, :])
    posB = f(128.0 * (np.arange(16)[None, :] - 12) + p_ - 256.0)
    posS = f(128.0 * (np.arange(33)[None, :] - 32) + p_)
    shared = dict(w_in=f(w_in[0]), w_mem=f(w_mem_kv[0]), w_oa=f(w_oa[0]), w_ob=f(w_ob[0]), w_om=f(w_om[0]),
                  w_out=f(w_out[0]), w_up=f(w_up[0]), w_down=f(w_down[0]), g4=g4, bgT=bgT, wcT=wcT, bcT=bcT,
                  gsub=gsub, lamq=lamq, lamk=lamk, relT=relT, rbfar=rbfar, ident=ident, kmq=kmq, posB=posB, posS=posS)
    in_maps = []
    for c in range(8):
        sl = slice(2 * c, 2 * c + 2)
        m = dict(shared)
        m.update(xp=f(x_prompt[sl]), xs=f(x_sample[sl]), memp=f(mem_prompt[sl]),
                 cak=f(np.asarray(cache_a_k[0, sl]).reshape(2, 512, 512)), cav=f(np.asarray(cache_a_v[0, sl]).reshape(2, 512, 512)),
                 cbk=f(np.asarray(cache_b_k[0, sl]).reshape(2, LPAST, 512)), cbv=f(np.asarray(cache_b_v[0, sl]).reshape(2, LPAST, 512)),
                 cmk=f(np.asarray(cache_mem_k[0, sl]).reshape(2, 256, 512)), cmv=f(np.asarray(cache_mem_v[0, sl]).reshape(2, 256, 512)),
                 sconvT=f(np.asarray(state_ffn_conv[0, sl]).reshape(2, 2, NFF, 128).transpose(3, 0, 2, 1)))
        in_maps.append(m)
    res = run_bass_kernel_spmd(nc, in_maps, core_ids=list(range(8)))
    R = res.results
    cat = lambda k: np.concatenate([np.asarray(r[k], dtype=np.float32) for r in R], axis=0)
    y_p = cat("y_p"); y_s = cat("y_s")
    outs = (y_p, y_s,
            cat("pa_k").reshape(1, 16, 512, 8, 64), cat("pa_v").reshape(1, 16, 512, 8, 64),
            cat("pb_k").reshape(1, 16, SEQ, 4, 128), cat("pb_v").reshape(1, 16, SEQ, 4, 128),
            cat("pm_k").reshape(1, 16, 256, 4, 128), cat("pm_v").reshape(1, 16, 256, 4, 128),
            cat("pconv").reshape(1, 16, 2, DFF),
            cat("sa_k").reshape(1, 16, 512, 8, 64), cat("sa_v").reshape(1, 16, 512, 8, 64),
            cat("sb_k").reshape(1, 16, 64, 4, 128), cat("sb_v").reshape(1, 16, 64, 4, 128),
            cat("sconv").reshape(1, 16, 2, DFF))
    return outs
```
